# Optimizing a Trainium2 kernel written in Bass

```python
import math
import jax, jax.numpy as jnp
from jax import lax
import numpy as np

D_MODEL = 1024
BATCH = 8
SEQ = 2048
DEPTH = 1

MLA_HEADS = 8
QK_NOPE_DIM = 64
QK_ROPE_DIM = 32
QK_HEAD_DIM = QK_NOPE_DIM + QK_ROPE_DIM
V_HEAD_DIM = 64
Q_LORA_RANK = 256
KV_LORA_RANK = 128
ROPE_THETA = 10000.0
Q_BLOCK = 128
HYENA_WIDTH = 512
HYENA_GROUPS = 8
FILTER_EMB = 33
FILTER_ORDER = 64
FAST_DECAY_PCT = 0.3
SLOW_DECAY_PCT = 1.5
DECAY_TARGET = 1e-2
PEER_HEADS = 8
N_KEYS = 128
N_EXPERTS = N_KEYS * N_KEYS
PEER_KEY_DIM = 256
PEER_TOPK = 16
PEER_TOKEN_BLOCK = 128
GATE_WIDTH = 2 * D_MODEL
IN_SPLITS = (Q_LORA_RANK, KV_LORA_RANK + QK_ROPE_DIM, 3 * HYENA_WIDTH, GATE_WIDTH)
IN_COLS = sum(IN_SPLITS)
EPS = 1e-6

kernel_name = 'hybrid_mla_hyena_peer_encoder'


def rms_norm(x, g):
    xf = x.astype(jnp.float32)
    y = xf * lax.rsqrt(jnp.mean(xf * xf, axis=-1, keepdims=True) + EPS)
    return (y * g.astype(jnp.float32)).astype(x.dtype)


def rotary(x):
    S = x.shape[1]
    half = QK_ROPE_DIM // 2
    inv_freq = ROPE_THETA ** (-jnp.arange(half, dtype=jnp.float32) / half)
    ang = jnp.arange(S, dtype=jnp.float32)[:, None] * inv_freq[None, :]
    cos = jnp.cos(ang)[None, :, None, :]
    sin = jnp.sin(ang)[None, :, None, :]
    x1 = x[..., :half].astype(jnp.float32)
    x2 = x[..., half:].astype(jnp.float32)
    out = jnp.concatenate([x1 * cos - x2 * sin, x2 * cos + x1 * sin], axis=-1)
    return out.astype(x.dtype)


def block_attention(q, k, v):
    B, S, H, Dh = q.shape
    nb = S // Q_BLOCK
    scale = QK_HEAD_DIM ** -0.5
    qb = q.reshape(B, nb, Q_BLOCK, H, Dh).transpose(1, 0, 2, 3, 4)

    def one_block(qi):
        s = jnp.einsum('bqhd,bkhd->bhqk', qi, k, preferred_element_type=jnp.float32) * scale
        p = jax.nn.softmax(s, axis=-1).astype(v.dtype)
        return jnp.einsum('bhqk,bkhd->bqhd', p, v)

    o = lax.map(one_block, qb)
    return o.transpose(1, 0, 2, 3, 4).reshape(B, S, H * V_HEAD_DIM)


def mla_branch(c_q, ckv_pe, q_a_norm, w_uq, kv_a_norm, w_ukv, q_norm, k_norm):
    B, S, _ = c_q.shape
    c_q = rms_norm(c_q, q_a_norm)
    q = (c_q @ w_uq).reshape(B, S, MLA_HEADS, QK_HEAD_DIM)
    c_kv = rms_norm(ckv_pe[..., :KV_LORA_RANK], kv_a_norm)
    k_pe = ckv_pe[..., KV_LORA_RANK:]
    kv = (c_kv @ w_ukv).reshape(B, S, MLA_HEADS, QK_NOPE_DIM + V_HEAD_DIM)
    k_nope, v = kv[..., :QK_NOPE_DIM], kv[..., QK_NOPE_DIM:]
    k = jnp.concatenate([k_nope, jnp.broadcast_to(k_pe[:, :, None, :], (B, S, MLA_HEADS, QK_ROPE_DIM))], axis=-1)
    q = rms_norm(q, q_norm)
    k = rms_norm(k, k_norm)
    q = jnp.concatenate([q[..., :QK_NOPE_DIM], rotary(q[..., QK_NOPE_DIM:])], axis=-1)
    k = jnp.concatenate([k[..., :QK_NOPE_DIM], rotary(k[..., QK_NOPE_DIM:])], axis=-1)
    return block_attention(q, k, v)


def short_conv(u, w, b):
    S = u.shape[1]
    up = jnp.pad(u, ((0, 0), (1, 1), (0, 0)))
    return up[:, :S] * w[0] + up[:, 1:S + 1] * w[1] + up[:, 2:] * w[2] + b


def implicit_filters(L, w1, b1, w2, b2, w3, b3, w4, b4, freq):
    f32 = jnp.float32
    t = jnp.linspace(0.0, 1.0, L, dtype=f32)[:, None]
    bands = (FILTER_EMB - 1) // 2
    w = 2.0 * math.pi * jnp.arange(L, dtype=f32)[:, None] / L
    f = jnp.linspace(1e-4, bands - 1, bands, dtype=f32)[None, :]
    z = jnp.concatenate([t, jnp.cos(f * w), -jnp.sin(f * w)], axis=-1)
    fr = freq.astype(f32)
    h = jnp.sin(fr * (z @ w1.astype(f32) + b1.astype(f32)))
    h = jnp.sin(fr * (h @ w2.astype(f32) + b2.astype(f32)))
    h = jnp.sin(fr * (h @ w3.astype(f32) + b3.astype(f32)))
    h = h @ w4.astype(f32) + b4.astype(f32)
    min_decay = math.log(DECAY_TARGET) / SLOW_DECAY_PCT
    max_decay = math.log(DECAY_TARGET) / FAST_DECAY_PCT
    deltas = jnp.abs(jnp.linspace(min_decay, max_decay, HYENA_WIDTH, dtype=f32))
    decay = jnp.exp(-t * deltas[None, :])
    h = h.reshape(L, 2, HYENA_WIDTH) * decay[:, None, :]
    h_fwd, h_bwd = h[:, 0], h[:, 1]
    return jnp.concatenate([h_fwd, jnp.zeros((1, HYENA_WIDTH), f32), h_bwd[1:][::-1]], axis=0)


def long_conv(z, k_two, bias):
    L = z.shape[1]
    zf = jnp.fft.rfft(z.astype(jnp.float32), n=2 * L, axis=1)
    kf = jnp.fft.rfft(k_two, n=2 * L, axis=0)
    y = jnp.fft.irfft(zf * kf[None], n=2 * L, axis=1)[:, :L]
    return (y + z.astype(jnp.float32) * bias.astype(jnp.float32)).astype(z.dtype)


def hyena_branch(u, conv_w, conv_b, w1, b1, w2, b2, w3, b3, w4, b4, freq, bias):
    L = u.shape[1]
    uc = short_conv(u, conv_w, conv_b)
    x0, x1, v = jnp.split(uc, 3, axis=-1)
    k_two = implicit_filters(L, w1, b1, w2, b2, w3, b3, w4, b4, freq)
    z = long_conv(v * x1, k_two, bias)
    return x0 * z


def peer(xn, w_q, keys1, keys2, expert_u, expert_v):
    B, S, D = xn.shape
    nb = (B * S) // PEER_TOKEN_BLOCK
    xt = xn.reshape(nb, PEER_TOKEN_BLOCK, D)
    half = PEER_KEY_DIM // 2

    def one_block(xb):
        q = (xb @ w_q).reshape(PEER_TOKEN_BLOCK, PEER_HEADS, 2, half)
        s1 = jnp.einsum('thd,hkd->thk', q[:, :, 0], keys1, preferred_element_type=jnp.float32)
        s2 = jnp.einsum('thd,hkd->thk', q[:, :, 1], keys2, preferred_element_type=jnp.float32)
        v1, i1 = lax.top_k(s1, PEER_TOPK)
        v2, i2 = lax.top_k(s2, PEER_TOPK)
        cand = (v1[..., :, None] + v2[..., None, :]).reshape(PEER_TOKEN_BLOCK, PEER_HEADS, PEER_TOPK * PEER_TOPK)
        cand_idx = (i1[..., :, None] * N_KEYS + i2[..., None, :]).reshape(PEER_TOKEN_BLOCK, PEER_HEADS, PEER_TOPK * PEER_TOPK)
        top_s, pos = lax.top_k(cand, PEER_TOPK)
        idx = jnp.take_along_axis(cand_idx, pos, axis=-1)
        g = jax.nn.softmax(top_s, axis=-1)
        u = expert_u[idx]
        act = jax.nn.gelu(jnp.einsum('thkd,td->thk', u, xb), approximate=False)
        wgt = (g * act).astype(xb.dtype)
        vv = expert_v[idx]
        return jnp.einsum('thk,thkd->td', wgt, vv)

    y = lax.map(one_block, xt)
    return y.reshape(B, S, D)


def setup_inputs(seed: int = 0) -> dict:
    key = jax.random.key(seed)
    ks = jax.random.split(key, 32)
    nrm = lambda k, shape, s: jax.random.normal(k, shape, jnp.float32) * s
    gain = lambda k, n: 1.0 + 0.02 * jax.random.normal(k, (DEPTH, n), jnp.float32)
    Ld = DEPTH
    return {
        'x': nrm(ks[0], (BATCH, SEQ, D_MODEL), 1.0),
        'attn_norm': gain(ks[1], D_MODEL),
        'w_in': nrm(ks[2], (Ld, D_MODEL, IN_COLS), D_MODEL ** -0.5),
        'b_gate': nrm(ks[3], (Ld, GATE_WIDTH), 0.02),
        'q_a_norm': gain(ks[4], Q_LORA_RANK),
        'w_uq': nrm(ks[5], (Ld, Q_LORA_RANK, MLA_HEADS * QK_HEAD_DIM), Q_LORA_RANK ** -0.5),
        'kv_a_norm': gain(ks[6], KV_LORA_RANK),
        'w_ukv': nrm(ks[7], (Ld, KV_LORA_RANK, MLA_HEADS * (QK_NOPE_DIM + V_HEAD_DIM)), KV_LORA_RANK ** -0.5),
        'q_norm': gain(ks[8], QK_HEAD_DIM),
        'k_norm': gain(ks[9], QK_HEAD_DIM),
        'w_o_attn': nrm(ks[10], (Ld, MLA_HEADS * V_HEAD_DIM, D_MODEL), (MLA_HEADS * V_HEAD_DIM) ** -0.5),
        'hyena_conv_w': nrm(ks[11], (Ld, 3, 3 * HYENA_WIDTH), 3 ** -0.5),
        'hyena_conv_b': nrm(ks[12], (Ld, 3 * HYENA_WIDTH), 0.02),
        'filt_w1': nrm(ks[13], (Ld, FILTER_EMB, FILTER_ORDER), FILTER_EMB ** -0.5),
        'filt_b1': nrm(ks[14], (Ld, FILTER_ORDER), 0.1),
        'filt_w2': nrm(ks[15], (Ld, FILTER_ORDER, FILTER_ORDER), FILTER_ORDER ** -0.5),
        'filt_b2': nrm(ks[16], (Ld, FILTER_ORDER), 0.1),
        'filt_w3': nrm(ks[17], (Ld, FILTER_ORDER, FILTER_ORDER), FILTER_ORDER ** -0.5),
        'filt_b3': nrm(ks[18], (Ld, FILTER_ORDER), 0.1),
        'filt_w4': nrm(ks[19], (Ld, FILTER_ORDER, 2 * HYENA_WIDTH), 0.125 * FILTER_ORDER ** -0.5),
        'filt_b4': nrm(ks[20], (Ld, 2 * HYENA_WIDTH), 0.01),
        'filt_freq': gain(ks[21], FILTER_ORDER),
        'hyena_bias': nrm(ks[22], (Ld, HYENA_WIDTH), 1.0),
        'w_o_hyena': nrm(ks[23], (Ld, HYENA_WIDTH, D_MODEL), HYENA_WIDTH ** -0.5),
        'w_out': nrm(ks[24], (Ld, D_MODEL, D_MODEL), D_MODEL ** -0.5),
        'ffn_norm': gain(ks[25], D_MODEL),
        'peer_w_q': nrm(ks[26], (Ld, D_MODEL, PEER_HEADS * PEER_KEY_DIM), D_MODEL ** -0.5),
        'peer_keys1': nrm(ks[27], (Ld, PEER_HEADS, N_KEYS, PEER_KEY_DIM // 2), (PEER_KEY_DIM // 2) ** -0.5),
        'peer_keys2': nrm(ks[28], (Ld, PEER_HEADS, N_KEYS, PEER_KEY_DIM // 2), (PEER_KEY_DIM // 2) ** -0.5),
        'expert_u': nrm(ks[29], (Ld, N_EXPERTS, D_MODEL), D_MODEL ** -0.5),
        'expert_v': nrm(ks[30], (Ld, N_EXPERTS, D_MODEL), (PEER_HEADS * PEER_TOPK) ** -0.5),
    }


def reference(x, attn_norm, w_in, b_gate, q_a_norm, w_uq, kv_a_norm, w_ukv, q_norm, k_norm, w_o_attn,
              hyena_conv_w, hyena_conv_b, filt_w1, filt_b1, filt_w2, filt_b2, filt_w3, filt_b3, filt_w4, filt_b4,
              filt_freq, hyena_bias, w_o_hyena, w_out, ffn_norm, peer_w_q, peer_keys1, peer_keys2, expert_u, expert_v):
    offs = list(np.cumsum(IN_SPLITS)[:-1])
    h = x
    for layer in range(DEPTH):
        xn = rms_norm(h, attn_norm[layer])
        proj = xn @ w_in[layer]
        c_q, ckv_pe, u_hy, gate_logits = jnp.split(proj, offs, axis=-1)
        gates = jax.nn.sigmoid((gate_logits + b_gate[layer]).astype(jnp.float32))
        g_attn, g_hy = gates[..., :D_MODEL], gates[..., D_MODEL:]
        a = mla_branch(c_q, ckv_pe, q_a_norm[layer], w_uq[layer], kv_a_norm[layer], w_ukv[layer],
                       q_norm[layer], k_norm[layer]) @ w_o_attn[layer]
        y_hy = hyena_branch(u_hy, hyena_conv_w[layer], hyena_conv_b[layer], filt_w1[layer], filt_b1[layer],
                            filt_w2[layer], filt_b2[layer], filt_w3[layer], filt_b3[layer], filt_w4[layer],
                            filt_b4[layer], filt_freq[layer], hyena_bias[layer]) @ w_o_hyena[layer]
        merged = (g_attn * a + g_hy * y_hy).astype(h.dtype)
        h = h + merged @ w_out[layer]
        hn = rms_norm(h, ffn_norm[layer])
        h = h + peer(hn, peer_w_q[layer], peer_keys1[layer], peer_keys2[layer], expert_u[layer], expert_v[layer])
    return h
```

```python
import math
import os
from contextlib import ExitStack

import numpy as np
import ml_dtypes

import concourse.bass as bass
import concourse.mybir as mybir
from concourse.bass_utils import run_bass_kernel_spmd

F32 = mybir.dt.float32
F32R = mybir.dt.float32r
BF16 = mybir.dt.bfloat16
I32 = mybir.dt.int32
U32 = mybir.dt.uint32
AF = mybir.ActivationFunctionType
ALU = mybir.AluOpType
AXX = mybir.AxisListType.X

ENG_NAMES = ("pe", "act", "dve", "pool", "sp")
EPS = 1e-6
S = 2048
NT = 16
PI = math.pi


class Buf:
    __slots__ = ("name", "w", "r", "excl")

    def __init__(self, name):
        self.name = name
        self.w = None
        self.r = []
        self.excl = False


class Prog:
    def __init__(self, nc, n_dma_sems=12):
        self.nc = nc
        self.streams = {e: [] for e in ENG_NAMES}
        self.cnt = {e: 0 for e in ENG_NAMES}
        self.seen = {e: {} for e in ENG_NAMES}
        self.n_dma_sems = n_dma_sems
        self.dma_i = {"sp": 0, "pool": 0, "act": 0}
        self.all_tokens = {}

    def _deps(self, eng, reads, writes):
        toks = []
        for b in reads:
            if b.w is not None:
                toks.append(b.w)
            if b.excl:
                toks.extend(tk for tk in b.r if tk[2] != eng)
        for b in writes:
            if b.w is not None:
                toks.append(b.w)
            toks.extend(b.r)
        need = {}
        for (k, v, e) in toks:
            if e == eng and eng == "pe":
                continue
            if self.seen[eng].get(k, 0) >= v:
                continue
            if need.get(k, 0) < v:
                need[k] = v
        for k, v in need.items():
            self.seen[eng][k] = v
        return list(need.items())

    def _commit(self, tok, reads, writes):
        for b in reads:
            b.r.append(tok)
            if len(b.r) > 64:
                best = {}
                for (k, v, e) in b.r:
                    if k not in best or best[k][1] < v:
                        best[k] = (k, v, e)
                b.r = list(best.values())
        for b in writes:
            b.w = tok
            b.r = []

    def op(self, eng, fn, reads=(), writes=()):
        waits = self._deps(eng, reads, writes)
        self.cnt[eng] += 1
        tok = ("c_" + eng, self.cnt[eng], eng)
        self.all_tokens[tok[0]] = tok[1]
        self.streams[eng].append((waits, fn, (tok[0], 1)))
        self._commit(tok, reads, writes)
        return tok

    def dma(self, q, fn, reads=(), writes=()):
        i = self.dma_i[q]
        self.dma_i[q] += 1
        K = self.n_dma_sems
        key = "d_%s_%d" % (q, i % K)
        val = 16 * (i // K + 1)
        waits = self._deps(q, reads, writes)
        if i >= K and self.seen[q].get(key, 0) < val - 16:
            waits.append((key, val - 16))
            self.seen[q][key] = val - 16
        tok = (key, val, "dma_" + q)
        self.all_tokens[key] = val
        self.streams[q].append((waits, fn, (key, 16)))
        self._commit(tok, reads, writes)
        return tok

    def barrier(self):
        for e in ENG_NAMES:
            waits = []
            for k, v in self.all_tokens.items():
                if k == "c_pe" and e == "pe":
                    continue
                if self.seen[e].get(k, 0) < v:
                    waits.append((k, v))
                    self.seen[e][k] = v
            if waits:
                self.streams[e].append((waits, None, None))

    def emit(self):
        nc = self.nc
        keys = sorted(self.all_tokens.keys())
        sems = {}
        with ExitStack() as st:
            for k in keys:
                sems[k] = st.enter_context(nc.semaphore(k))
            block = st.enter_context(nc.Block())

            def run(ename):
                def body(engine):
                    for (waits, fn, inc) in self.streams[ename]:
                        for (k, v) in waits:
                            engine.wait_ge(sems[k], v)
                        if fn is not None:
                            ins = fn(engine)
                            ins.then_inc(sems[inc[0]], inc[1])
                    if ename == "sp":
                        for k, v in self.all_tokens.items():
                            engine.wait_ge(sems[k], v)
                return body

            block.tensor(run("pe"))
            block.scalar(run("act"))
            block.vector(run("dve"))
            block.gpsimd(run("pool"))
            block.sync(run("sp"))


class Tl:
    def __init__(self, t, nb, name):
        self.t = t
        self.b = [Buf("%s_%d" % (name, i)) for i in range(nb)]

    def __getitem__(self, k):
        return self.t[k]


_DT_SIZE = {F32: 4, BF16: 2, I32: 4, U32: 4}


class Arena:
    def __init__(self, nc, base=16384, limit=16384 + 212000):
        self.nc = nc
        self.off = base
        self.limit = limit
        self.peak = base
        self.n = 0

    def alloc(self, name, shape, dt, nb=1):
        nbytes = int(np.prod(shape[1:])) * _DT_SIZE[dt]
        nbytes = (nbytes + 63) // 64 * 64
        assert self.off + nbytes <= self.limit, ("SBUF overflow", name, self.off, nbytes)
        self.n += 1
        t = self.nc.alloc_sbuf_tensor_at("%s_%d" % (name, self.n), list(shape), dt, offset=self.off)
        self.off += nbytes
        self.peak = max(self.peak, self.off)
        return Tl(t, nb, name)

    def mark(self):
        return self.off

    def release(self, m):
        self.off = m


def bc(ap, axis, shape):
    return ap.unsqueeze(axis).broadcast_to(list(shape))


def build_nc(stop_after=99, dbg=()):
    nc = bass.Bass("TRN2", target_bir_lowering=False)

    def din(name, shape, dt=F32):
        return nc.dram_tensor(name, list(shape), dt, kind="ExternalInput").ap()

    x = din("x", [S, 1024])
    attn_norm = din("attn_norm", [1, 1024])
    w_in = din("w_in", [1024, 4000])
    b_gate = din("b_gate", [1, 2048])
    q_a_norm = din("q_a_norm", [1, 256])
    w_uq = din("w_uq", [256, 768])
    kv_a_norm = din("kv_a_norm", [1, 128])
    w_ukv = din("w_ukv", [128, 1024])
    q_norm = din("q_norm", [1, 96])
    k_norm = din("k_norm", [1, 96])
    w_o_attn = din("w_o_attn", [512, 1024])
    conv_w = din("conv_w", [128, 12, 3])
    conv_b = din("conv_b", [128, 12])
    filt_w1 = din("filt_w1", [33, 64])
    filt_w2 = din("filt_w2", [64, 64])
    filt_w3 = din("filt_w3", [64, 64])
    filt_b = din("filt_b", [64, 4])
    filt_w4a = din("filt_w4a", [65, 1024])
    hy_bias = din("hy_bias", [1, 512])
    w_o_hyena = din("w_o_hyena", [512, 1024])
    w_out = din("w_out", [1024, 1024])
    ffn_norm = din("ffn_norm", [1, 1024])
    peer_w_q = din("peer_w_q", [1024, 2048])
    keysT = din("keysT", [128, 16, 128])
    n_exp = 16384 if stop_after >= 7 else 128
    expert_u = din("expert_u", [n_exp, 1024])
    expert_v = din("expert_v", [n_exp, 1024])
    identb = din("identb", [128, 128], BF16)
    identf = din("identf", [128, 128])
    rot_cos = din("rot_cos", [S, 16])
    rot_sin = din("rot_sin", [S, 16])
    zfeat = din("zfeat", [33, S])
    decay_f = din("decay_f", [S, 512])
    decay_b = din("decay_b", [S, 512])
    dftC = din("dftC", [8, 128, 16, 256], BF16)
    dftS = din("dftS", [8, 128, 16, 256], BF16)
    phase = din("phase", [128, 16, 2])
    iota16 = din("iota16", [128, 256])
    out = nc.dram_tensor("out", [S, 1024], F32, kind="ExternalOutput").ap()
    uvb = nc.dram_tensor("uvb", [n_exp, 2048], BF16).ap()
    b_uvb = []
    dbg_out = {}

    p = Prog(nc)
    ar = Arena(nc)
    hnp = [nc.alloc_psum_tensor("hnp%d" % i, [128, 1024], F32) for i in range(2)]

    class PsView:
        def __init__(self, ap, name):
            self.ap = ap
            self.b = [Buf(name)]

        def __getitem__(self, k):
            return self.ap[k]

    ps = [PsView(hnp[i // 2][:, (i % 2) * 512:(i % 2 + 1) * 512], "ps%d" % i) for i in range(4)]
    ps += [Tl(nc.alloc_psum_tensor("ps%d" % i, [128, 512], F32), 1, "ps%d" % i) for i in range(4, 8)]

    for t_ in ps:
        t_.b[0].excl = True

    def psb(i):
        return ps[i][:].bitcast(BF16)

    b_out = [Buf("out%d" % t) for t in range(NT)]

    def dump(name, tl, shape, dt):
        if name in dbg:
            d = nc.dram_tensor("dbg_" + name, list(shape), dt, kind="ExternalOutput").ap()
            dbg_out[name] = d
            p.dma("sp", lambda e: e.dma_start(out=d, in_=tl[:]), reads=tl.b)

    idb = ar.alloc("idb", [128, 128], BF16)
    idf = ar.alloc("idf", [128, 128], F32)
    p.dma("sp", lambda e: e.dma_start(out=idb[:], in_=identb), writes=idb.b)
    p.dma("sp", lambda e: e.dma_start(out=idf[:], in_=identf), writes=idf.b)
    st = ar.alloc("st", [128, NT, 8], F32)
    junk = ar.alloc("junk", [128, 1024], F32)
    m_persist = ar.mark()

    xT = ar.alloc("xT", [128, 8, S], BF16, nb=NT)
    g1 = ar.alloc("g1", [128, 1024], F32)
    p.dma("sp", lambda e: e.dma_start(out=g1[:], in_=attn_norm.broadcast_to([128, 1024])), writes=g1.b)
    m1 = ar.mark()

    def norm_and_transpose(t, src_ap, src_bufs, gt, dstT_ap, dst_bufs, psi, i, xn_t=None):
        p.op("act", lambda e: e.activation(out=junk[:], in_=src_ap, func=AF.Square, accum_out=st[:, t, 0:1]),
             reads=src_bufs, writes=junk.b + st.b)
        p.op("act", lambda e: e.activation(out=st[:, t, 1:2], in_=st[:, t, 0:1], func=AF.Sqrt, scale=1.0 / 1024, bias=EPS),
             reads=st.b, writes=st.b)
        p.op("dve", lambda e: e.reciprocal(out=st[:, t, 2:3], in_=st[:, t, 1:2]), reads=st.b, writes=st.b)
        p.op("dve", lambda e: e.scalar_tensor_tensor(out=xn_t[:, i, :], in0=src_ap, scalar=st[:, t, 2:3], in1=gt[:],
                                                      op0=ALU.mult, op1=ALU.mult),
             reads=src_bufs + st.b + gt.b, writes=[xn_t.b[i]])
        pt = psb(psi)
        for k in range(8):
            p.op("pe", lambda e, k=k: e.transpose(out=pt[:, k * 128:(k + 1) * 128], in_=xn_t[:, i, k * 128:(k + 1) * 128], identity=idb[:]),
                 reads=[xn_t.b[i]] + idb.b, writes=ps[psi].b)
        p.op("act", lambda e: e.copy(out=dstT_ap, in_=pt.rearrange("p (k n) -> p k n", k=8)),
             reads=ps[psi].b, writes=dst_bufs)

    qT = ar.alloc("qT", [128, 8, S], BF16, nb=1)
    kT = ar.alloc("kT", [128, 8, S], BF16, nb=1)
    v_sb = ar.alloc("v_sb", [128, NT, 8, 68], BF16, nb=1)
    m2 = ar.mark()
    w_a = ar.alloc("w_a", [128, 8, 416], BF16)
    p.dma("pool", lambda e: e.dma_start(out=w_a[:], in_=w_in[:, 0:416].rearrange("(k p) n -> p k n", p=128)), writes=w_a.b)
    w_uq_sb = ar.alloc("w_uq", [128, 2, 768], BF16)
    p.dma("pool", lambda e: e.dma_start(out=w_uq_sb[:], in_=w_uq.rearrange("(k p) n -> p k n", p=128)), writes=w_uq_sb.b)
    w_ukv_sb = ar.alloc("w_ukv", [128, 1024], BF16)
    p.dma("pool", lambda e: e.dma_start(out=w_ukv_sb[:], in_=w_ukv), writes=w_ukv_sb.b)
    gq = ar.alloc("gq", [128, 256], F32)
    gkv = ar.alloc("gkv", [128, 128], F32)
    n96 = ar.alloc("n96", [128, 2, 96], F32)
    p.dma("sp", lambda e: e.dma_start(out=gq[:], in_=q_a_norm.broadcast_to([128, 256])), writes=gq.b)
    p.dma("sp", lambda e: e.dma_start(out=gkv[:], in_=kv_a_norm.broadcast_to([128, 128])), writes=gkv.b)
    p.dma("sp", lambda e: e.dma_start(out=n96[:, 0, :], in_=q_norm.broadcast_to([128, 96])), writes=n96.b)
    p.dma("sp", lambda e: e.dma_start(out=n96[:, 1, :], in_=k_norm.broadcast_to([128, 96])), writes=n96.b)
    g96 = ar.alloc("g96", [128, 2, 8, 96], F32)
    p.op("dve", lambda e: e.tensor_scalar(out=g96[:, 0, :, :], in0=bc(n96[:, 0, :], 1, [128, 8, 96]), scalar1=96.0 ** -0.5,
                                          scalar2=None, op0=ALU.mult), reads=n96.b, writes=g96.b)
    p.op("dve", lambda e: e.tensor_copy(out=g96[:, 1, :, :], in_=bc(n96[:, 1, :], 1, [128, 8, 96])), reads=n96.b, writes=g96.b)
    rcos = ar.alloc("rcos", [128, NT, 16], F32)
    rsin = ar.alloc("rsin", [128, NT, 16], F32)
    p.dma("sp", lambda e: e.dma_start(out=rcos[:], in_=rot_cos.rearrange("(n p) d -> p n d", p=128)), writes=rcos.b)
    p.dma("sp", lambda e: e.dma_start(out=rsin[:], in_=rot_sin.rearrange("(n p) d -> p n d", p=128)), writes=rsin.b)
    p.op("dve", lambda e: e.memset(v_sb[:, :, :, 64:65], 1.0), writes=v_sb.b)
    xt = ar.alloc("xt", [128, 2, 1024], F32, nb=2)
    xn = ar.alloc("xn", [128, 2, 1024], BF16, nb=2)
    cqn = ar.alloc("cqn", [128, 2, 384], BF16, nb=2)
    cT = ar.alloc("cT", [128, 2, 3, 128], BF16, nb=2)
    qs = ar.alloc("qs", [128, 2, 2, 768], F32, nb=2)
    sq = ar.alloc("sq", [128, 1536], F32)
    hst = ar.alloc("hst", [128, 4, 16], F32)
    qn = ar.alloc("qn", [128, 2, 8, 96], F32)
    rt = ar.alloc("rt", [128, 4, 16, 16], F32)
    qr = ar.alloc("qr", [128, 2, 8, 96], BF16)
    kpe = ar.alloc("kpe", [128, 2, 32], F32, nb=2)


    def headnorm(t):
        se = t % 2
        src = qs[:, se, :, :].rearrange("p w f -> p (w f)")
        src3 = qs[:, se, :, :].rearrange("p w (h d) -> p (w h) d", h=8)
        bsrc = [qs.b[se]]
        qn3 = qn[:].rearrange("p w h d -> p (w h) d")
        g3 = g96[:].rearrange("p w h d -> p (w h) d")
        qr3 = qr[:].rearrange("p w h d -> p (w h) d")
        p.op("act", lambda e: e.activation(out=sq[:], in_=src, func=AF.Square), reads=bsrc, writes=sq.b)
        p.op("dve", lambda e: e.tensor_reduce(out=hst[:, 0, :], in_=sq[:].rearrange("p (h d) -> p h d", h=16), axis=AXX, op=ALU.add),
             reads=sq.b, writes=hst.b)
        p.op("act", lambda e: e.activation(out=hst[:, 1, :], in_=hst[:, 0, :], func=AF.Sqrt, scale=1.0 / 96, bias=EPS),
             reads=hst.b, writes=hst.b)
        p.op("dve", lambda e: e.reciprocal(out=hst[:, 2, :], in_=hst[:, 1, :]), reads=hst.b, writes=hst.b)
        p.op("dve", lambda e: e.tensor_tensor(out=qn3, in0=src3, in1=bc(hst[:, 2, :], 2, [128, 16, 96]), op=ALU.mult),
             reads=bsrc + hst.b, writes=qn.b)
        p.op("dve", lambda e: e.tensor_tensor(out=qn3, in0=qn3, in1=g3, op=ALU.mult),
             reads=qn.b + g96.b, writes=qn.b)
        x1 = qn3[:, :, 64:80]
        x2 = qn3[:, :, 80:96]
        cosb = bc(rcos[:, t, :], 1, [128, 16, 16])
        sinb = bc(rsin[:, t, :], 1, [128, 16, 16])
        p.op("dve", lambda e: e.tensor_tensor(out=rt[:, 0, :, :], in0=x1, in1=cosb, op=ALU.mult), reads=qn.b + rcos.b, writes=rt.b)
        p.op("dve", lambda e: e.tensor_tensor(out=rt[:, 1, :, :], in0=x2, in1=sinb, op=ALU.mult), reads=qn.b + rsin.b, writes=rt.b)
        p.op("dve", lambda e: e.tensor_tensor(out=rt[:, 2, :, :], in0=x2, in1=cosb, op=ALU.mult), reads=qn.b + rcos.b, writes=rt.b)
        p.op("dve", lambda e: e.tensor_tensor(out=rt[:, 3, :, :], in0=x1, in1=sinb, op=ALU.mult), reads=qn.b + rsin.b, writes=rt.b)
        p.op("dve", lambda e: e.tensor_tensor(out=qr3[:, :, 64:80], in0=rt[:, 0, :, :], in1=rt[:, 1, :, :], op=ALU.subtract),
             reads=rt.b, writes=qr.b)
        p.op("dve", lambda e: e.tensor_tensor(out=qr3[:, :, 80:96], in0=rt[:, 2, :, :], in1=rt[:, 3, :, :], op=ALU.add),
             reads=rt.b, writes=qr.b)
        p.op("act", lambda e: e.copy(out=qr3[:, :, 0:64], in_=qn3[:, :, 0:64]), reads=qn.b, writes=qr.b)
        for which, dstT in ((0, qT), (1, kT)):
            pt = psb(which)
            for h in range(8):
                p.op("pe", lambda e, h=h, which=which, pt=pt: e.transpose(out=pt[0:96, h * 128:(h + 1) * 128], in_=qr[:, which, h, :], identity=idb[:]),
                     reads=qr.b + idb.b, writes=ps[which].b)
            p.op("act", lambda e, pt=pt, dstT=dstT: e.copy(out=dstT[0:96, :, t * 128:(t + 1) * 128], in_=pt[0:96, :].rearrange("p (h n) -> p h n", h=8)),
                 reads=ps[which].b, writes=dstT.b)

    def p1a(t):
        i = t % 2
        p.dma("sp", lambda e: e.dma_start(out=xt[:, i, :], in_=x[t * 128:(t + 1) * 128, :]), writes=[xt.b[i]])
        norm_and_transpose(t, xt[:, i, :], [xt.b[i]], g1, xT[:, :, t * 128:(t + 1) * 128], [xT.b[t]], 3, i, xn_t=xn)

    def p1b_front(t):
        se = t % 2
        pa = ps[2]
        for k in range(8):
            p.op("pe", lambda e, k=k: e.matmul(pa[:, 0:416], lhsT=xT[:, k, t * 128:(t + 1) * 128], rhs=w_a[:, k, :],
                                              start=(k == 0), stop=(k == 7)),
                 reads=[xT.b[t]] + w_a.b, writes=pa.b)
        p.op("act", lambda e: e.activation(out=junk[:, 0:256], in_=pa[:, 0:256], func=AF.Square, accum_out=st[:, t, 3:4]),
             reads=pa.b, writes=junk.b + st.b)
        p.op("act", lambda e: e.activation(out=junk[:, 256:384], in_=pa[:, 256:384], func=AF.Square, accum_out=st[:, t, 4:5]),
             reads=pa.b, writes=junk.b + st.b)
        p.op("act", lambda e: e.activation(out=st[:, t, 5:6], in_=st[:, t, 3:4], func=AF.Sqrt, scale=1.0 / 256, bias=EPS),
             reads=st.b, writes=st.b)
        p.op("act", lambda e: e.activation(out=st[:, t, 6:7], in_=st[:, t, 4:5], func=AF.Sqrt, scale=1.0 / 128, bias=EPS),
             reads=st.b, writes=st.b)
        p.op("act", lambda e: e.copy(out=kpe[:, se, :], in_=pa[:, 384:416]), reads=pa.b, writes=[kpe.b[se]])
        p.op("dve", lambda e: e.reciprocal(out=st[:, t, 3:5], in_=st[:, t, 5:7]), reads=st.b, writes=st.b)
        p.op("dve", lambda e: e.scalar_tensor_tensor(out=cqn[:, se, 0:256], in0=pa[:, 0:256], scalar=st[:, t, 3:4], in1=gq[:],
                                                      op0=ALU.mult, op1=ALU.mult),
             reads=pa.b + st.b + gq.b, writes=[cqn.b[se]])
        p.op("dve", lambda e: e.scalar_tensor_tensor(out=cqn[:, se, 256:384], in0=pa[:, 256:384], scalar=st[:, t, 4:5], in1=gkv[:],
                                                      op0=ALU.mult, op1=ALU.mult),
             reads=pa.b + st.b + gkv.b, writes=[cqn.b[se]])
        pt5 = psb(5)
        for k in range(3):
            p.op("pe", lambda e, k=k: e.transpose(out=pt5[:, 512 + k * 128:512 + (k + 1) * 128], in_=cqn[:, se, k * 128:(k + 1) * 128], identity=idb[:]),
                 reads=[cqn.b[se]] + idb.b, writes=ps[5].b)
        p.op("act", lambda e: e.copy(out=cT[:, se, :, :], in_=pt5[:, 512:896].rearrange("p (k n) -> p k n", k=3)), reads=ps[5].b, writes=[cT.b[se]])
        for k in range(2):
            p.op("pe", lambda e, k=k: e.matmul(ps[4][:, 0:512], lhsT=cT[:, se, k, :], rhs=w_uq_sb[:, k, 0:512], start=(k == 0), stop=(k == 1)),
                 reads=[cT.b[se]] + w_uq_sb.b, writes=ps[4].b)
        for k in range(2):
            p.op("pe", lambda e, k=k: e.matmul(ps[5][:, 0:256], lhsT=cT[:, se, k, :], rhs=w_uq_sb[:, k, 512:768], start=(k == 0), stop=(k == 1)),
                 reads=[cT.b[se]] + w_uq_sb.b, writes=ps[5].b)
        p.op("pe", lambda e: e.matmul(ps[6][:, 0:512], lhsT=cT[:, se, 2, :], rhs=w_ukv_sb[:, 0:512], start=True, stop=True),
             reads=[cT.b[se]] + w_ukv_sb.b, writes=ps[6].b)
        p.op("pe", lambda e: e.matmul(ps[7][:, 0:512], lhsT=cT[:, se, 2, :], rhs=w_ukv_sb[:, 512:1024], start=True, stop=True),
             reads=[cT.b[se]] + w_ukv_sb.b, writes=ps[7].b)
        p.op("act", lambda e: e.copy(out=qs[:, se, 0, 0:512], in_=ps[4][:, 0:512]), reads=ps[4].b, writes=[qs.b[se]])
        p.op("act", lambda e: e.copy(out=qs[:, se, 0, 512:768], in_=ps[5][:, 0:256]), reads=ps[5].b, writes=[qs.b[se]])
        k3 = qs[:, se, 1, :].rearrange("p (h d) -> p h d", h=8)
        for hb_, pk in ((0, 6), (1, 7)):
            kvv = ps[pk][:, 0:512].rearrange("p (h d) -> p h d", h=4)
            p.op("act", lambda e, hb_=hb_, kvv=kvv: e.copy(out=k3[:, hb_ * 4:(hb_ + 1) * 4, 0:64], in_=kvv[:, :, 0:64]),
                 reads=ps[pk].b, writes=[qs.b[se]])
            p.op("act", lambda e, hb_=hb_, kvv=kvv: e.copy(out=v_sb[:, t, hb_ * 4:(hb_ + 1) * 4, 0:64], in_=kvv[:, :, 64:128]),
                 reads=ps[pk].b, writes=v_sb.b)
        p.op("dve", lambda e: e.tensor_copy(out=k3[:, :, 64:96], in_=bc(kpe[:, se, :], 1, [128, 8, 32])), reads=[kpe.b[se]], writes=[qs.b[se]])

    def record(fn, *args):
        rec = []
        orig_op, orig_dma = p.op, p.dma
        p.op = lambda *a, **k: rec.append((orig_op, a, k))
        p.dma = lambda *a, **k: rec.append((orig_dma, a, k))
        try:
            fn(*args)
        finally:
            p.op, p.dma = orig_op, orig_dma
        return rec

    def interleave(*recs):
        items = []
        for ci, rec in enumerate(recs):
            n = len(rec)
            for i_, it in enumerate(rec):
                items.append(((i_ + 0.5) / n, ci, i_, it))
        items.sort(key=lambda z: (z[0], z[1], z[2]))
        for (_, _, _, (f_, a_, k_)) in items:
            f_(*a_, **k_)

    p1a(0)
    p1a(1)
    p1b_front(0)
    CRB = 1024
    for rb in range(n_exp // CRB if n_exp >= CRB else 0):
        for k_, src_ in enumerate((expert_u, expert_v)):
            bb_ = Buf("uvb%d_%d" % (rb, k_))
            b_uvb.append(bb_)
            p.dma("pool", lambda e, rb=rb, k_=k_, src_=src_: e.dma_start(out=uvb[rb * CRB:(rb + 1) * CRB, k_ * 1024:(k_ + 1) * 1024],
                                                                   in_=src_[rb * CRB:(rb + 1) * CRB, :]),
                  reads=([xT.b[1]] if (rb == 0 and k_ == 0) else []), writes=[bb_])
    for t in range(NT):
        recs = [record(headnorm, t)]
        if t + 1 < NT:
            recs.append(record(p1b_front, t + 1))
        if t + 2 < NT:
            recs.append(record(p1a, t + 2))
        interleave(*recs)
    dump("qT", qT, [128, 8, S], BF16)
    dump("kT", kT, [128, 8, S], BF16)
    dump("v_sb", v_sb, [128, NT, 8, 68], BF16)
    if stop_after <= 2:
        return finish(nc, p, dbg_out)

    p.barrier()
    ar.release(m2)
    zT = ar.alloc("zT", [128, NT, 512], BF16)
    x0s = ar.alloc("x0s", [128, 4, S], BF16)
    m3 = ar.mark()
    cw = ar.alloc("cw", [128, 12, 3], F32)
    cbv = ar.alloc("cbv", [128, 12], F32)
    p.dma("sp", lambda e: e.dma_start(out=cw[:], in_=conv_w), writes=cw.b)
    p.dma("sp", lambda e: e.dma_start(out=cbv[:], in_=conv_b), writes=cbv.b)
    wh = ar.alloc("wh", [128, 2, 8, 128], BF16, nb=2)
    ubuf = ar.alloc("ubuf", [128, S + 2], F32)
    cbA = ar.alloc("cbA", [128, S], F32)
    cbB = ar.alloc("cbB", [128, S], F32)
    zj = ar.alloc("zj", [128, S], BF16)
    p.op("dve", lambda e: e.memset(ubuf[:, 0:1], 0.0), writes=ubuf.b)
    p.op("dve", lambda e: e.memset(ubuf[:, S + 1:S + 2], 0.0), writes=ubuf.b)
    wi = 0
    deferred = []
    for j in range(4):
        for part, dst in ((1, cbA), (2, cbB), (0, None)):
            col0 = 416 + part * 512 + j * 128
            ci = part * 4 + j
            i = wi % 2
            wi += 1
            p.dma("pool", lambda e, i=i, col0=col0: e.dma_start(out=wh[:, i, :, :], in_=w_in[:, col0:col0 + 128].rearrange("(k p) n -> p k n", p=128)),
                  writes=[wh.b[i]])
            for tb in range(4):
                for k in range(8):
                    p.op("pe", lambda e, i=i, k=k, tb=tb: e.matmul(ps[tb][:, :], lhsT=wh[:, i, k, :], rhs=xT[:, k, tb * 512:(tb + 1) * 512],
                                                               start=(k == 0), stop=(k == 7)),
                         reads=[wh.b[i]] + xT.b[tb * 4:(tb + 1) * 4], writes=ps[tb].b)
                p.op("act", lambda e, tb=tb: e.copy(out=ubuf[:, 1 + tb * 512:1 + (tb + 1) * 512], in_=ps[tb][:, :]), reads=ps[tb].b, writes=ubuf.b)
            while deferred:
                deferred.pop(0)()
            if dst is None:
                dst = cbA
            p.op("dve", lambda e, dst=dst, ci=ci: e.tensor_scalar(out=dst[:], in0=ubuf[:, 1:S + 1], scalar1=cw[:, ci, 1:2], scalar2=cbv[:, ci:ci + 1],
                                                                  op0=ALU.mult, op1=ALU.add),
                 reads=ubuf.b + cw.b + cbv.b, writes=dst.b)
            p.op("dve", lambda e, dst=dst, ci=ci: e.scalar_tensor_tensor(out=dst[:], in0=ubuf[:, 0:S], scalar=cw[:, ci, 0:1], in1=dst[:],
                                                                         op0=ALU.mult, op1=ALU.add),
                 reads=ubuf.b + cw.b + dst.b, writes=dst.b)
            p.op("dve", lambda e, dst=dst, ci=ci: e.scalar_tensor_tensor(out=dst[:], in0=ubuf[:, 2:S + 2], scalar=cw[:, ci, 2:3], in1=dst[:],
                                                                         op0=ALU.mult, op1=ALU.add),
                 reads=ubuf.b + cw.b + dst.b, writes=dst.b)
            if part == 2:
                p.op("dve", lambda e: e.tensor_tensor(out=zj[:], in0=cbA[:], in1=cbB[:], op=ALU.mult), reads=cbA.b + cbB.b, writes=zj.b)

                def z_transposes(j=j):
                  for half in range(2):
                    pt = psb(4 + half)
                    for q_ in range(8):
                        tcn = half * 8 + q_
                        p.op("pe", lambda e, q_=q_, tcn=tcn, pt=pt: e.transpose(out=pt[:, q_ * 128:(q_ + 1) * 128], in_=zj[:, tcn * 128:(tcn + 1) * 128], identity=idb[:]),
                             reads=zj.b + idb.b, writes=ps[4 + half].b)
                    p.op("act", lambda e, half=half, pt=pt, j=j: e.copy(out=zT[:, half * 8:(half + 1) * 8, j * 128:(j + 1) * 128],
                                                                    in_=pt.rearrange("p (k n) -> p k n", k=8)),
                         reads=ps[4 + half].b, writes=zT.b)
                deferred.append(z_transposes)
            if part == 0:
                p.op("act", lambda e, j=j: e.copy(out=x0s[:, j, :], in_=cbA[:]), reads=cbA.b, writes=x0s.b)
    dump("zT", zT, [128, NT, 512], BF16)
    dump("x0s", x0s, [128, 4, S], BF16)
    if stop_after <= 3:
        return finish(nc, p, dbg_out)

    p.barrier()
    ar.release(m3)
    attn_o = ar.alloc("attn_o", [128, NT, 512], BF16)
    m4 = ar.mark()
    PT = ar.alloc("PT", [128, 2, NT, 512], BF16, nb=2 * NT)
    rc = ar.alloc("rc", [128, 4], F32, nb=4)
    def attn_S(n_):
        h, qg = divmod(n_, 4)
        bi = n_ % 2
        for kc in range(NT):
            sp_ = ps[kc % 2]
            p.op("pe", lambda e, kc=kc, sp_=sp_: e.matmul(sp_[:, :], lhsT=kT[0:96, h, kc * 128:(kc + 1) * 128],
                                                      rhs=qT[0:96, h, qg * 512:(qg + 1) * 512], start=True, stop=True),
                 reads=kT.b + qT.b, writes=sp_.b)
            p.op("act", lambda e, kc=kc, sp_=sp_: e.activation(out=PT[:, bi, kc, :], in_=sp_[:, :], func=AF.Exp),
                 reads=sp_.b, writes=[PT.b[bi * NT + kc]])

    def attn_PV(n_):
        h, qg = divmod(n_, 4)
        bi = n_ % 2
        for qt in range(4):
            oi = n_ * 4 + qt
            po = ps[2 + oi % 2]
            ri = oi % 4
            tile_i = qg * 4 + qt
            for kc in range(NT):
                p.op("pe", lambda e, kc=kc, qt=qt, po=po: e.matmul(po[:, 0:65], lhsT=PT[:, bi, kc, qt * 128:(qt + 1) * 128],
                                                               rhs=v_sb[:, kc, h, 0:65], start=(kc == 0), stop=(kc == NT - 1)),
                     reads=[PT.b[bi * NT + kc]] + v_sb.b, writes=po.b)
            p.op("dve", lambda e, ri=ri, po=po: e.reciprocal(out=rc[:, ri:ri + 1], in_=po[:, 64:65]), reads=po.b, writes=[rc.b[ri]])
            p.op("dve", lambda e, ri=ri, po=po, tile_i=tile_i: e.tensor_scalar(out=attn_o[:, tile_i, h * 64:(h + 1) * 64], in0=po[:, 0:64],
                                                                          scalar1=rc[:, ri:ri + 1], scalar2=None, op0=ALU.mult),
                 reads=po.b + [rc.b[ri]], writes=attn_o.b)

    attn_S(0)
    for n_ in range(32):
        recs = [record(attn_PV, n_)]
        if n_ + 1 < 32:
            recs.append(record(attn_S, n_ + 1))
        interleave(*recs)
    dump("attn_o", attn_o, [128, NT, 512], BF16)
    if stop_after <= 4:
        return finish(nc, p, dbg_out)

    p.barrier()
    ar.release(m4)
    lo = Arena(nc, base=m_persist, limit=m2)
    lo.n = 1000
    ab = lo.alloc("ab", [128, 2, NT, 512], BF16)
    ml = lo.mark()
    zf = lo.alloc("zf", [64, S], F32)
    hA = lo.alloc("hA", [128, S + 1], F32)
    hB = lo.alloc("hB", [128, S + 1], F32)
    pre = lo.alloc("pre", [64, S], F32)
    kf = lo.alloc("kf", [64, S], F32)
    ki = lo.alloc("ki", [64, S], I32)
    w1s = lo.alloc("w1s", [64, 64], F32)
    w2s = lo.alloc("w2s", [64, 64], F32)
    w3s = lo.alloc("w3s", [64, 64], F32)
    fbs = lo.alloc("fbs", [64, 8], F32)
    w4s = lo.alloc("w4s", [128, 1024], F32)
    p.dma("sp", lambda e: e.dma_start(out=zf[0:33, :], in_=zfeat), writes=zf.b)
    p.dma("sp", lambda e: e.dma_start(out=w1s[0:33, :], in_=filt_w1), writes=w1s.b)
    p.dma("sp", lambda e: e.dma_start(out=w2s[:], in_=filt_w2), writes=w2s.b)
    p.dma("sp", lambda e: e.dma_start(out=w3s[:], in_=filt_w3), writes=w3s.b)
    p.dma("sp", lambda e: e.dma_start(out=fbs[:, 0:4], in_=filt_b), writes=fbs.b)
    p.dma("sp", lambda e: e.dma_start(out=w4s[0:65, :], in_=filt_w4a), writes=w4s.b)
    p.op("dve", lambda e: e.tensor_scalar(out=fbs[:, 4:7], in0=fbs[:, 0:3], scalar1=fbs[:, 3:4], scalar2=None, op0=ALU.mult),
         reads=fbs.b, writes=fbs.b)
    for hh in (hA, hB):
        p.op("dve", lambda e, hh=hh: e.memset(hh[0:64, S:S + 1], 0.0), writes=hh.b)
        p.op("dve", lambda e, hh=hh: e.memset(hh[64:65, :], 1.0), writes=hh.b)

    def sin_layer(l, wt, kdim, src, dst):
        for tb in range(4):
            p.op("pe", lambda e, tb=tb: e.matmul(ps[tb][0:64, :], lhsT=wt[0:kdim, :], rhs=src[0:kdim, tb * 512:(tb + 1) * 512], start=True, stop=True),
                 reads=wt.b + src.b, writes=ps[tb].b)
            p.op("dve", lambda e, tb=tb: e.tensor_scalar(out=pre[:, tb * 512:(tb + 1) * 512], in0=ps[tb][0:64, :], scalar1=fbs[:, 3:4],
                                                         scalar2=fbs[:, 4 + l:5 + l], op0=ALU.mult, op1=ALU.add),
                 reads=ps[tb].b + fbs.b, writes=pre.b)
        p.op("dve", lambda e: e.tensor_scalar(out=ki[:], in0=pre[:], scalar1=1.0 / (2 * PI), scalar2=8.5, op0=ALU.mult, op1=ALU.add),
             reads=pre.b, writes=ki.b)
        p.op("dve", lambda e: e.tensor_copy(out=kf[:], in_=ki[:]), reads=ki.b, writes=kf.b)
        p.op("dve", lambda e: e.tensor_scalar(out=kf[:], in0=kf[:], scalar1=-8.0, scalar2=-2 * PI, op0=ALU.add, op1=ALU.mult),
             reads=kf.b, writes=kf.b)
        p.op("dve", lambda e: e.tensor_tensor(out=pre[:], in0=pre[:], in1=kf[:], op=ALU.add), reads=pre.b + kf.b, writes=pre.b)
        p.op("dve", lambda e: e.tensor_scalar(out=kf[:], in0=pre[:], scalar1=-PI, scalar2=None, op0=ALU.is_lt), reads=pre.b, writes=kf.b)
        p.op("dve", lambda e: e.scalar_tensor_tensor(out=pre[:], in0=kf[:], scalar=2 * PI, in1=pre[:], op0=ALU.mult, op1=ALU.add),
             reads=pre.b + kf.b, writes=pre.b)
        p.op("dve", lambda e: e.tensor_scalar(out=kf[:], in0=pre[:], scalar1=PI, scalar2=None, op0=ALU.is_gt), reads=pre.b, writes=kf.b)
        p.op("dve", lambda e: e.scalar_tensor_tensor(out=pre[:], in0=kf[:], scalar=-2 * PI, in1=pre[:], op0=ALU.mult, op1=ALU.add),
             reads=pre.b + kf.b, writes=pre.b)
        p.op("dve", lambda e: e.tensor_scalar(out=pre[:], in0=pre[:], scalar1=-3.141592, scalar2=3.141592, op0=ALU.max, op1=ALU.min),
             reads=pre.b, writes=pre.b)
        p.op("act", lambda e: e.activation(out=dst[0:64, 0:S], in_=pre[:], func=AF.Sin), reads=pre.b, writes=dst.b)

    sin_layer(0, w1s, 33, zf, hA)
    sin_layer(1, w2s, 64, hA, hB)
    sin_layer(2, w3s, 64, hB, hA)
    dump("h3", hA, [128, S + 1], F32)
    dec = lo.alloc("dec", [128, 2, 2, 512], F32, nb=2)
    t12 = lo.alloc("t12", [128, 2, 2, 512], F32, nb=2)
    brow = lo.alloc("brow", [128, 512], F32)
    p.dma("sp", lambda e: e.dma_start(out=brow[0:1, :], in_=hy_bias), writes=brow.b)
    for n_ in range(NT):
        i = n_ % 2
        pf, pb_ = ps[4 + 2 * i], ps[5 + 2 * i]
        p.op("pe", lambda e, n_=n_, pf=pf: e.matmul(pf[:, :], lhsT=hA[0:65, n_ * 128:(n_ + 1) * 128], rhs=w4s[0:65, 0:512], start=True, stop=True),
             reads=hA.b + w4s.b, writes=pf.b)
        p.op("pe", lambda e, n_=n_, pb_=pb_: e.matmul(pb_[:, :], lhsT=hA[0:65, n_ * 128 + 1:(n_ + 1) * 128 + 1], rhs=w4s[0:65, 512:1024], start=True, stop=True),
             reads=hA.b + w4s.b, writes=pb_.b)
        p.dma("sp", lambda e, n_=n_, i=i: e.dma_start(out=dec[:, i, 0, :], in_=decay_f[n_ * 128:(n_ + 1) * 128, :]), writes=[dec.b[i]])
        p.dma("sp", lambda e, n_=n_, i=i: e.dma_start(out=dec[:, i, 1, :], in_=decay_b[n_ * 128:(n_ + 1) * 128, :]), writes=[dec.b[i]])
        p.op("dve", lambda e, i=i, pf=pf: e.tensor_tensor(out=t12[:, i, 0, :], in0=pf[:, :], in1=dec[:, i, 0, :], op=ALU.mult),
             reads=pf.b + [dec.b[i]], writes=[t12.b[i]])
        p.op("dve", lambda e, i=i, pb_=pb_: e.tensor_tensor(out=t12[:, i, 1, :], in0=pb_[:, :], in1=dec[:, i, 1, :], op=ALU.mult),
             reads=pb_.b + [dec.b[i]], writes=[t12.b[i]])
        if n_ == 0:
            p.op("dve", lambda e, i=i: e.tensor_tensor(out=t12[0:1, i, 0, :], in0=t12[0:1, i, 0, :], in1=brow[0:1, :], op=ALU.add),
                 reads=[t12.b[i]] + brow.b, writes=[t12.b[i]])
        p.op("dve", lambda e, i=i, n_=n_: e.tensor_tensor(out=ab[:, 0, n_, :], in0=t12[:, i, 0, :], in1=t12[:, i, 1, :], op=ALU.add),
             reads=[t12.b[i]], writes=ab.b)
        p.op("dve", lambda e, i=i, n_=n_: e.tensor_tensor(out=ab[:, 1, n_, :], in0=t12[:, i, 0, :], in1=t12[:, i, 1, :], op=ALU.subtract),
             reads=[t12.b[i]], writes=ab.b)
    dump("ab", ab, [128, 2, NT, 512], BF16)
    p.barrier()
    lo.release(ml)
    Yr = lo.alloc("Yr", [128, 2, NT, 512], BF16)
    tabs = lo.alloc("tabs", [128, 2, 2, NT, 256], BF16, nb=2)
    phs = lo.alloc("phs", [128, NT, 2], F32)
    p.dma("sp", lambda e: e.dma_start(out=phs[:], in_=phase), writes=phs.b)
    kk = ar.alloc("kk", [128, 2, 4, 512], F32, nb=2)
    yy = ar.alloc("yy", [128, 2, 4, 512], F32, nb=2)
    for cb in range(8):
        ti = cb % 2
        p.dma("sp", lambda e, cb=cb, ti=ti: e.dma_start(out=tabs[:, ti, 0, :, :], in_=dftC[cb]), writes=[tabs.b[ti]])
        p.dma("sp", lambda e, cb=cb, ti=ti: e.dma_start(out=tabs[:, ti, 1, :, :], in_=dftS[cb]), writes=[tabs.b[ti]])
        for half in range(2):
            fc = cb * 2 + half
            pi_ = fc % 2
            pz = [ps[4 * pi_ + q_] for q_ in range(4)]
            for q_, (cs_, src, si) in enumerate(((0, zT, None), (1, zT, None), (0, ab, 0), (1, ab, 1))):
                for sc in range(NT):
                    rhs = zT[:, sc, :] if si is None else ab[:, si, sc, :]
                    p.op("pe", lambda e, q_=q_, cs_=cs_, sc=sc, rhs=rhs, ti=ti, half=half, pz=pz: e.matmul(
                        pz[q_][:, :], lhsT=tabs[:, ti, cs_, sc, half * 128:(half + 1) * 128], rhs=rhs, start=(sc == 0), stop=(sc == NT - 1)),
                         reads=[tabs.b[ti]] + src.b, writes=pz[q_].b)
            Zc, Zs, Kc, Ks = pz
            pcs = phs[:, fc, 0:1]
            pss = phs[:, fc, 1:2]
            bk = [kk.b[pi_]]
            by = [yy.b[pi_]]
            p.op("dve", lambda e, Kc=Kc, pcs=pcs, pi_=pi_: e.tensor_scalar(out=kk[:, pi_, 0, :], in0=Kc[:, :], scalar1=pcs, scalar2=None, op0=ALU.mult),
                 reads=Kc.b + phs.b, writes=bk)
            p.op("dve", lambda e, Ks=Ks, pss=pss, pi_=pi_: e.scalar_tensor_tensor(out=kk[:, pi_, 1, :], in0=Ks[:, :], scalar=pss, in1=kk[:, pi_, 0, :],
                                                                             op0=ALU.mult, op1=ALU.add),
                 reads=Ks.b + phs.b + bk, writes=bk)
            p.op("dve", lambda e, Ks=Ks, pcs=pcs, pi_=pi_: e.tensor_scalar(out=kk[:, pi_, 2, :], in0=Ks[:, :], scalar1=pcs, scalar2=None, op0=ALU.mult),
                 reads=Ks.b + phs.b, writes=bk)
            p.op("dve", lambda e, Kc=Kc, pss=pss, pi_=pi_: e.scalar_tensor_tensor(out=kk[:, pi_, 3, :], in0=Kc[:, :], scalar=pss, in1=kk[:, pi_, 2, :],
                                                                             op0=ALU.mult, op1=ALU.subtract),
                 reads=Kc.b + phs.b + bk, writes=bk)
            p.op("dve", lambda e, Zc=Zc, pi_=pi_: e.tensor_tensor(out=yy[:, pi_, 0, :], in0=Zc[:, :], in1=kk[:, pi_, 1, :], op=ALU.mult),
                 reads=Zc.b + bk, writes=by)
            p.op("dve", lambda e, Zs=Zs, pi_=pi_: e.tensor_tensor(out=yy[:, pi_, 1, :], in0=Zs[:, :], in1=kk[:, pi_, 3, :], op=ALU.mult),
                 reads=Zs.b + bk, writes=by)
            p.op("dve", lambda e, Zs=Zs, pi_=pi_: e.tensor_tensor(out=yy[:, pi_, 2, :], in0=Zs[:, :], in1=kk[:, pi_, 1, :], op=ALU.mult),
                 reads=Zs.b + bk, writes=by)
            p.op("dve", lambda e, Zc=Zc, pi_=pi_: e.tensor_tensor(out=yy[:, pi_, 3, :], in0=Zc[:, :], in1=kk[:, pi_, 3, :], op=ALU.mult),
                 reads=Zc.b + bk, writes=by)
            p.op("dve", lambda e, pi_=pi_, fc=fc: e.tensor_tensor(out=Yr[:, 0, fc, :], in0=yy[:, pi_, 0, :], in1=yy[:, pi_, 1, :], op=ALU.add),
                 reads=by, writes=Yr.b)
            p.op("dve", lambda e, pi_=pi_, fc=fc: e.tensor_tensor(out=Yr[:, 1, fc, :], in0=yy[:, pi_, 2, :], in1=yy[:, pi_, 3, :], op=ALU.subtract),
                 reads=by, writes=Yr.b)
    dump("Yr", Yr, [128, 2, NT, 512], BF16)
    yh = x0s
    bi_ = 0
    for tb in range(8):
        ti = tb % 2
        p.dma("sp", lambda e, tb=tb, ti=ti: e.dma_start(out=tabs[:, ti, 0, :, :], in_=dftC[tb]), writes=[tabs.b[ti]])
        p.dma("sp", lambda e, tb=tb, ti=ti: e.dma_start(out=tabs[:, ti, 1, :, :], in_=dftS[tb]), writes=[tabs.b[ti]])
        for cj in range(4):
            po = ps[bi_ % 4]
            bi_ += 1
            n_mm = 0
            NB_ = int(os.environ.get("DBG_B", "32"))
            for fc in range(NT):
                for cs_ in range(2):
                    if n_mm >= NB_:
                        continue
                    p.op("pe", lambda e, fc=fc, cs_=cs_, cj=cj, ti=ti, po=po, n_mm=n_mm: e.matmul(
                        po[:, 0:256], lhsT=Yr[:, cs_, fc, cj * 128:(cj + 1) * 128], rhs=tabs[:, ti, cs_, fc, :],
                        start=(n_mm == 0), stop=(n_mm == NB_ - 1)),
                         reads=Yr.b + [tabs.b[ti]], writes=po.b)
                    n_mm += 1
            if os.environ.get("DBG_Y"):
                p.op("dve", lambda e, po=po, cj=cj, tb=tb: e.tensor_copy(out=yh[:, cj, tb * 256:(tb + 1) * 256], in_=po[:, 0:256]),
                     reads=po.b + x0s.b, writes=yh.b)
                continue
            p.op("dve", lambda e, po=po, cj=cj, tb=tb: e.tensor_tensor(out=yh[:, cj, tb * 256:(tb + 1) * 256], in0=po[:, 0:256],
                                                                       in1=x0s[:, cj, tb * 256:(tb + 1) * 256], op=ALU.mult),
                 reads=po.b + x0s.b, writes=yh.b)
    dump("yh", yh, [128, 4, S], BF16)
    if stop_after <= 5:
        return finish(nc, p, dbg_out)

    p.barrier()
    lo = Arena(nc, base=m_persist, limit=m2)
    lo.n = 2000
    w_g = lo.alloc("w_g", [128, 8, 2048], BF16, nb=4)
    for c4 in range(4):
        p.dma("pool", lambda e, c4=c4: e.dma_start(out=w_g[:, :, c4 * 512:(c4 + 1) * 512],
                                                   in_=w_in[:, 1952 + c4 * 512:1952 + (c4 + 1) * 512].rearrange("(k p) n -> p k n", p=128)),
              writes=[w_g.b[c4]])
    w_o = lo.alloc("w_o", [128, 8, 1024], BF16, nb=2)
    for c2 in range(2):
        p.dma("pool", lambda e, c2=c2: e.dma_start(out=w_o[:, :, c2 * 512:(c2 + 1) * 512],
                                                   in_=w_out[:, c2 * 512:(c2 + 1) * 512].rearrange("(k p) n -> p k n", p=128)), writes=[w_o.b[c2]])
    w_oa = lo.alloc("w_oa", [128, 4, 1024], BF16)
    w_oh = lo.alloc("w_oh", [128, 4, 1024], BF16)
    p.dma("pool", lambda e: e.dma_start(out=w_oa[:], in_=w_o_attn.rearrange("(k p) n -> p k n", p=128)), writes=w_oa.b)
    p.dma("pool", lambda e: e.dma_start(out=w_oh[:], in_=w_o_hyena.rearrange("(k p) n -> p k n", p=128)), writes=w_oh.b)
    bg = lo.alloc("bg", [128, 2048], BF16)
    p.dma("pool", lambda e: e.dma_start(out=bg[0:1, :], in_=b_gate), writes=bg.b)
    ones = lo.alloc("ones", [128, 128], BF16)
    p.op("dve", lambda e: e.memset(ones[:], 1.0), writes=ones.b)
    g1b = lo.alloc("g1b", [128, 1024], F32)
    p.dma("sp", lambda e: e.dma_start(out=g1b[:], in_=attn_norm.broadcast_to([128, 1024])), writes=g1b.b)
    xt4 = lo.alloc("xt4", [128, 3, 1024], F32, nb=3)
    xn4 = lo.alloc("xn4", [128, 2, 1024], BF16, nb=2)
    xnT = lo.alloc("xnT", [128, 2, 8, 128], BF16, nb=2)
    aoT = lo.alloc("aoT", [128, 2, 4, 128], BF16, nb=2)
    sg = lo.alloc("sg", [128, 2, 512], F32, nb=2)
    mm_ = lo.alloc("mm", [128, 1024], F32)
    tmpm = lo.alloc("tmpm", [128, 2, 512], F32, nb=2)
    mb = lo.alloc("mb", [128, 2, 1024], BF16, nb=2)
    mT = lo.alloc("mT", [128, 8, 128], BF16)
    ho = lo.alloc("ho", [128, 1, 1024], F32, nb=1)

    def p4_front_a(t):
        i = t % 2
        i3 = t % 3
        src_ap = xt4[:, i3, :]
        p.dma("sp", lambda e: e.dma_start(out=xt4[:, i3, :], in_=x[t * 128:(t + 1) * 128, :]), writes=[xt4.b[i3]])
        p.op("act", lambda e: e.activation(out=junk[:], in_=src_ap, func=AF.Square, accum_out=st[:, t, 0:1]),
             reads=[xt4.b[i3]], writes=junk.b + st.b)
        p.op("act", lambda e: e.activation(out=st[:, t, 1:2], in_=st[:, t, 0:1], func=AF.Sqrt, scale=1.0 / 1024, bias=EPS),
             reads=st.b, writes=st.b)
        p.op("dve", lambda e: e.reciprocal(out=st[:, t, 2:3], in_=st[:, t, 1:2]), reads=st.b, writes=st.b)
        p.op("dve", lambda e: e.scalar_tensor_tensor(out=xn4[:, i, :], in0=src_ap, scalar=st[:, t, 2:3], in1=g1b[:],
                                                      op0=ALU.mult, op1=ALU.mult),
             reads=[xt4.b[i3]] + st.b + g1b.b, writes=[xn4.b[i]])

    def p4_front(t):
        i = t % 2
        pt0 = psb(0)
        for k in range(8):
            p.op("pe", lambda e, k=k: e.transpose(out=pt0[:, k * 128:(k + 1) * 128], in_=xn4[:, i, k * 128:(k + 1) * 128], identity=idb[:]),
                 reads=[xn4.b[i]] + idb.b, writes=ps[0].b)
        p.op("act", lambda e: e.copy(out=xnT[:, i, :, :], in_=pt0.rearrange("p (k n) -> p k n", k=8)), reads=ps[0].b, writes=[xnT.b[i]])
        pt1 = psb(1)
        for k4 in range(4):
            p.op("pe", lambda e, k4=k4: e.transpose(out=pt1[:, k4 * 128:(k4 + 1) * 128], in_=attn_o[:, t, k4 * 128:(k4 + 1) * 128], identity=idb[:]),
                 reads=attn_o.b + idb.b, writes=ps[1].b)
        p.op("act", lambda e: e.copy(out=aoT[:, i, :, :], in_=pt1[:, 0:512].rearrange("p (k n) -> p k n", k=4)), reads=ps[1].b, writes=[aoT.b[i]])

    def p4_mid(t):
        i = t % 2
        for br in range(2):
            for half in range(2):
                gcol = br * 1024 + half * 512
                pg = ps[2 + half]
                pv = ps[4 + half]
                for k in range(8):
                    p.op("pe", lambda e, k=k, gcol=gcol, pg=pg: e.matmul(pg[:, :], lhsT=xnT[:, i, k, :], rhs=w_g[:, k, gcol:gcol + 512], start=(k == 0), stop=False),
                         reads=[xnT.b[i], w_g.b[gcol // 512]], writes=pg.b)
                p.op("pe", lambda e, gcol=gcol, pg=pg: e.matmul(pg[:, :], lhsT=ones[0:1, :], rhs=bg[0:1, gcol:gcol + 512], start=False, stop=True),
                     reads=ones.b + bg.b, writes=pg.b)
                for k4 in range(4):
                    if br == 0:
                        p.op("pe", lambda e, k4=k4, half=half, pv=pv: e.matmul(pv[:, :], lhsT=aoT[:, i, k4, :], rhs=w_oa[:, k4, half * 512:(half + 1) * 512],
                                                                        start=(k4 == 0), stop=(k4 == 3)),
                             reads=[aoT.b[i]] + w_oa.b, writes=pv.b)
                    else:
                        p.op("pe", lambda e, k4=k4, half=half, pv=pv: e.matmul(pv[:, :], lhsT=yh[:, k4, t * 128:(t + 1) * 128],
                                                                        rhs=w_oh[:, k4, half * 512:(half + 1) * 512], start=(k4 == 0), stop=(k4 == 3)),
                             reads=yh.b + w_oh.b, writes=pv.b)
                p.op("act", lambda e, half=half, pg=pg: e.activation(out=sg[:, half, :], in_=pg[:, :], func=AF.Sigmoid), reads=pg.b, writes=[sg.b[half]])
                if br == 0:
                    p.op("dve", lambda e, half=half, pv=pv: e.tensor_tensor(out=mm_[:, half * 512:(half + 1) * 512], in0=pv[:, :], in1=sg[:, half, :], op=ALU.mult),
                         reads=pv.b + [sg.b[half]], writes=mm_.b)
                else:
                    p.op("dve", lambda e, half=half, pv=pv: e.tensor_tensor(out=tmpm[:, half, :], in0=pv[:, :], in1=sg[:, half, :], op=ALU.mult),
                         reads=pv.b + [sg.b[half]], writes=[tmpm.b[half]])
                    p.op("dve", lambda e, half=half: e.tensor_tensor(out=mb[:, i, half * 512:(half + 1) * 512], in0=mm_[:, half * 512:(half + 1) * 512],
                                                                     in1=tmpm[:, half, :], op=ALU.add),
                         reads=mm_.b + [tmpm.b[half]], writes=[mb.b[i]])

    def p4_back(t):
        i = t % 2
        pt6 = psb(6)
        for k in range(8):
            p.op("pe", lambda e, k=k: e.transpose(out=pt6[:, k * 128:(k + 1) * 128], in_=mb[:, i, k * 128:(k + 1) * 128], identity=idb[:]),
                 reads=[mb.b[i]] + idb.b, writes=ps[6].b)
        p.op("act", lambda e: e.copy(out=mT[:], in_=pt6.rearrange("p (k n) -> p k n", k=8)), reads=ps[6].b, writes=mT.b)
        for half in range(2):
            pw = ps[7]
            for k in range(8):
                p.op("pe", lambda e, k=k, half=half, pw=pw: e.matmul(pw[:, :], lhsT=mT[:, k, :], rhs=w_o[:, k, half * 512:(half + 1) * 512],
                                                              start=(k == 0), stop=(k == 7)),
                     reads=mT.b + [w_o.b[half]], writes=pw.b)
            p.op("dve", lambda e, half=half, pw=pw: e.tensor_tensor(out=ho[:, 0, half * 512:(half + 1) * 512], in0=pw[:, :],
                                                               in1=xt4[:, t % 3, half * 512:(half + 1) * 512], op=ALU.add),
                 reads=pw.b + [xt4.b[t % 3]], writes=ho.b)
        p.dma("sp", lambda e: e.dma_start(out=out[t * 128:(t + 1) * 128, :], in_=ho[:, 0, :]), reads=ho.b, writes=[b_out[t]])

    def p4_fm(t):
        p4_front(t)
        p4_mid(t)

    p4_front_a(0)
    p4_front_a(1)
    p4_fm(0)
    for t in range(NT):
        recs = [record(p4_back, t)]
        if t + 1 < NT:
            recs.append(record(p4_fm, t + 1))
        if t + 2 < NT:
            recs.append(record(p4_front_a, t + 2))
        interleave(*recs)
    if stop_after <= 6:
        return finish(nc, p, dbg_out)

    p.barrier()
    ar = Arena(nc, base=m_persist)
    ar.n = 3000
    w_q = ar.alloc("w_q", [128, 8, 2048], BF16, nb=4)
    for c4 in range(4):
        p.dma("pool", lambda e, c4=c4: e.dma_start(out=w_q[:, :, c4 * 512:(c4 + 1) * 512],
                                                   in_=peer_w_q[:, c4 * 512:(c4 + 1) * 512].rearrange("(k p) n -> p k n", p=128)), writes=[w_q.b[c4]])
    kTs = ar.alloc("kTs", [128, 16, 128], F32)
    p.dma("sp", lambda e: e.dma_start(out=kTs[:], in_=keysT), writes=kTs.b)
    g2 = ar.alloc("g2", [128, 1024], F32)
    p.dma("sp", lambda e: e.dma_start(out=g2[:], in_=ffn_norm.broadcast_to([128, 1024])), writes=g2.b)
    io16 = ar.alloc("io16", [128, 256], F32)
    p.dma("sp", lambda e: e.dma_start(out=io16[:], in_=iota16), writes=io16.b)
    ht = ar.alloc("ht", [128, 2, 1024], F32, nb=2)
    ei = ar.alloc("ei", [128, 2, 128], I32, nb=2)
    gw = ar.alloc("gw", [128, 2, 128], F32, nb=2)
    hnb2 = ar.alloc("hnb2", [128, 2, 1024], BF16, nb=2)
    prod = None
    junka = None
    hnT = ar.alloc("hnT", [128, 8, 128], BF16)
    qTs = ar.alloc("qTs", [128, 16, 128], F32)
    sc = ar.alloc("sc", [128, 16, 128], F32)
    scw = ar.alloc("scw", [128, 16, 128], F32, nb=16)
    v16 = ar.alloc("v16", [128, 16, 16], F32, nb=16)
    i16u = ar.alloc("i16u", [128, 16, 16], U32, nb=16)
    i16f = ar.alloc("i16f", [128, 16, 16], F32)
    cand = ar.alloc("cand", [128, 8, 256], F32)
    candw = ar.alloc("candw", [128, 8, 256], F32, nb=8)
    ts_ = ar.alloc("ts", [128, 8, 16], F32, nb=8)
    posu = ar.alloc("posu", [128, 8, 16], U32, nb=8)
    phu = ar.alloc("phu", [128, 2, 8, 16], U32)
    phf = ar.alloc("phf", [128, 2, 8, 16], F32)
    oh = ar.alloc("oh", [128, 8, 16, 16], F32)
    sel = ar.alloc("sel", [128, 2, 8, 16], F32)
    ef = ar.alloc("ef", [128, 128], F32)
    sm = ar.alloc("sm", [128, 4, 8], F32)
    ex = ar.alloc("ex", [128, 8, 16], F32)
    da = ar.alloc("da", [128, 2, 128], F32, nb=4)
    dsum = ar.alloc("dsum", [128, 128], F32, nb=16)
    actv = ar.alloc("actv", [128, 128], F32, nb=16)
    wgt = ar.alloc("wgt", [128, 128], F32, nb=16)
    junkb = ar.alloc("junkb", [128, 4, 512], BF16, nb=4)
    NB = 18
    ub = ar.alloc("ub", [128, NB, 2048], BF16, nb=NB)
    ND = 8
    dg = ar.alloc("dg", [128, ND, 128], BF16, nb=ND)
    fo = junk

    def make_steps(t):
        i = t % 2
        hp = [ps[2 * i], ps[2 * i + 1]]
        src = ht[:, i, :]
        pt = psb(6)
        v4 = v16[:].rearrange("p (h s) k -> p h s k", s=2)
        i4 = i16f[:].rearrange("p (h s) k -> p h s k", s=2)
        cand4 = cand[:].rearrange("p h (a b) -> p h a b", a=16)

        def s0():
            p.dma("sp", lambda e: e.dma_start(out=ht[:, i, :], in_=out[t * 128:(t + 1) * 128, :]), reads=[b_out[t]], writes=[ht.b[i]])
            p.op("act", lambda e: e.activation(out=junk[:], in_=src, func=AF.Square, accum_out=st[:, t, 0:1]), reads=[ht.b[i]], writes=junk.b + st.b)
            p.op("act", lambda e: e.activation(out=st[:, t, 1:2], in_=st[:, t, 0:1], func=AF.Sqrt, scale=1.0 / 1024, bias=EPS), reads=st.b, writes=st.b)

        def s1():
            p.op("dve", lambda e: e.reciprocal(out=st[:, t, 2:3], in_=st[:, t, 1:2]), reads=st.b, writes=st.b)
            for half in range(2):
                p.op("dve", lambda e, half=half: e.scalar_tensor_tensor(out=hp[half][:, :], in0=ht[:, i, half * 512:(half + 1) * 512], scalar=st[:, t, 2:3],
                                                                       in1=g2[:, half * 512:(half + 1) * 512], op0=ALU.mult, op1=ALU.mult),
                     reads=[ht.b[i]] + st.b + g2.b, writes=hp[half].b)
                p.op("dve", lambda e, half=half: e.tensor_copy(out=hnb2[:, i, half * 512:(half + 1) * 512], in_=hp[half][:, :]), reads=hp[half].b, writes=[hnb2.b[i]])

        def s2():
            for k in range(8):
                p.op("pe", lambda e, k=k: e.transpose(out=pt[:, k * 128:(k + 1) * 128], in_=hnb2[:, i, k * 128:(k + 1) * 128], identity=idb[:]),
                     reads=[hnb2.b[i]] + idb.b, writes=ps[6].b)

        def s3():
            p.op("act", lambda e: e.copy(out=hnT[:], in_=pt.rearrange("p (k n) -> p k n", k=8)), reads=ps[6].b, writes=hnT.b)

        def qmm(rnd):
            def f():
                for c8 in range(8):
                    c = rnd * 8 + c8
                    pq = ps[6 + c8 // 4]
                    for k in range(8):
                        p.op("pe", lambda e, c=c, c8=c8, k=k, pq=pq: e.matmul(pq[:, (c8 % 4) * 128:(c8 % 4 + 1) * 128], lhsT=w_q[:, k, c * 128:(c + 1) * 128],
                                                                      rhs=hnT[:, k, :], start=(k == 0), stop=(k == 7)),
                             reads=w_q.b + hnT.b, writes=pq.b)
            return f

        def qcp(rnd, dst):
            def f():
                for b2 in range(2):
                    p.op("act", lambda e, b2=b2: e.copy(out=dst[:, rnd * 8 + b2 * 4:rnd * 8 + (b2 + 1) * 4, :],
                                                        in_=ps[6 + b2][:, :].rearrange("p (c n) -> p c n", c=4)),
                         reads=ps[6 + b2].b, writes=dst.b)
            return f

        def smm(rnd):
            def f():
                for c8 in range(8):
                    c = rnd * 8 + c8
                    pq = ps[6 + c8 // 4]
                    p.op("pe", lambda e, c=c, c8=c8, pq=pq: e.matmul(pq[:, (c8 % 4) * 128:(c8 % 4 + 1) * 128], lhsT=qTs[:, c, :], rhs=kTs[:, c, :], start=True, stop=True),
                         reads=qTs.b + kTs.b, writes=pq.b)
            return f

        def k1():
            for c in range(16):
                p.op("dve", lambda e, c=c: e.max(out=v16[:, c, 0:8], in_=sc[:, c, :]), reads=sc.b, writes=[v16.b[c]])
            for c in range(16):
                p.op("dve", lambda e, c=c: e.max_index(out=i16u[:, c, 0:8], in_max=v16[:, c, 0:8], in_values=sc[:, c, :]), reads=sc.b + [v16.b[c]], writes=[i16u.b[c]])

        def k2():
            for c in range(16):
                p.op("dve", lambda e, c=c: e.match_replace(out=scw[:, c, :], in_to_replace=v16[:, c, 0:8], in_values=sc[:, c, :], imm_value=-1e30),
                     reads=sc.b + [v16.b[c]], writes=[scw.b[c]])
            for c in range(16):
                p.op("dve", lambda e, c=c: e.max(out=v16[:, c, 8:16], in_=scw[:, c, :]), reads=[scw.b[c]], writes=[v16.b[c]])

        def k3():
            for c in range(16):
                p.op("dve", lambda e, c=c: e.max_index(out=i16u[:, c, 8:16], in_max=v16[:, c, 8:16], in_values=scw[:, c, :]), reads=[scw.b[c], v16.b[c]], writes=[i16u.b[c]])
            p.op("dve", lambda e: e.tensor_copy(out=i16f[:], in_=i16u[:]), reads=i16u.b, writes=i16f.b)
            p.op("dve", lambda e: e.tensor_tensor(out=cand4, in0=bc(v4[:, :, 0, :], 3, [128, 8, 16, 16]), in1=bc(v4[:, :, 1, :], 2, [128, 8, 16, 16]), op=ALU.add),
                 reads=v16.b, writes=cand.b)

        def k4():
            for h in range(8):
                p.op("dve", lambda e, h=h: e.max(out=ts_[:, h, 0:8], in_=cand[:, h, :]), reads=cand.b, writes=[ts_.b[h]])
            for h in range(8):
                p.op("dve", lambda e, h=h: e.max_index(out=posu[:, h, 0:8], in_max=ts_[:, h, 0:8], in_values=cand[:, h, :]), reads=cand.b + [ts_.b[h]], writes=[posu.b[h]])
            for h in range(8):
                p.op("dve", lambda e, h=h: e.match_replace(out=candw[:, h, :], in_to_replace=ts_[:, h, 0:8], in_values=cand[:, h, :], imm_value=-1e30),
                     reads=cand.b + [ts_.b[h]], writes=[candw.b[h]])

        def k5():
            for h in range(8):
                p.op("dve", lambda e, h=h: e.max(out=ts_[:, h, 8:16], in_=candw[:, h, :]), reads=[candw.b[h]], writes=[ts_.b[h]])
            for h in range(8):
                p.op("dve", lambda e, h=h: e.max_index(out=posu[:, h, 8:16], in_max=ts_[:, h, 8:16], in_values=candw[:, h, :]), reads=[candw.b[h], ts_.b[h]], writes=[posu.b[h]])
            p.op("dve", lambda e: e.tensor_scalar(out=phu[:, 0, :, :], in0=posu[:], scalar1=4, scalar2=None, op0=ALU.logical_shift_right), reads=posu.b, writes=phu.b)
            p.op("dve", lambda e: e.tensor_scalar(out=phu[:, 1, :, :], in0=posu[:], scalar1=15, scalar2=None, op0=ALU.bitwise_and), reads=posu.b, writes=phu.b)
            p.op("dve", lambda e: e.tensor_copy(out=phf[:], in_=phu[:]), reads=phu.b, writes=phf.b)
            p.op("dve", lambda e: e.tensor_tensor(out=ex[:], in0=ts_[:], in1=bc(ts_[:, :, 0], 2, [128, 8, 16]), op=ALU.subtract), reads=ts_.b, writes=ex.b)
            p.op("act", lambda e: e.activation(out=ex[:], in_=ex[:], func=AF.Exp), reads=ex.b, writes=ex.b)

        def k6():
            for s_ in range(2):
                p.op("dve", lambda e, s_=s_: e.tensor_tensor(out=oh[:], in0=bc(phf[:, s_, :, :], 3, [128, 8, 16, 16]),
                                                             in1=bc(io16[:].rearrange("p (a b) -> p a b", a=16), 1, [128, 8, 16, 16]), op=ALU.is_equal),
                     reads=phf.b + io16.b, writes=oh.b)
                p.op("dve", lambda e, s_=s_: e.tensor_tensor(out=oh[:], in0=oh[:], in1=bc(i4[:, :, s_, :], 2, [128, 8, 16, 16]), op=ALU.mult),
                     reads=oh.b + i16f.b, writes=oh.b)
                p.op("dve", lambda e, s_=s_: e.tensor_reduce(out=sel[:, s_, :, :], in_=oh[:], axis=AXX, op=ALU.add), reads=oh.b, writes=sel.b)
            p.op("dve", lambda e: e.scalar_tensor_tensor(out=ef[:], in0=sel[:, 0, :, :].rearrange("p h k -> p (h k)"), scalar=128.0,
                                                          in1=sel[:, 1, :, :].rearrange("p h k -> p (h k)"), op0=ALU.mult, op1=ALU.add),
                 reads=sel.b, writes=ef.b)
            p.op("dve", lambda e: e.tensor_copy(out=ei[:, i, :], in_=ef[:]), reads=ef.b, writes=[ei.b[i]])
            p.op("dve", lambda e: e.tensor_reduce(out=sm[:, 0, :], in_=ex[:], axis=AXX, op=ALU.add), reads=ex.b, writes=sm.b)
            p.op("dve", lambda e: e.reciprocal(out=sm[:, 1, :], in_=sm[:, 0, :]), reads=sm.b, writes=sm.b)
            p.op("dve", lambda e: e.tensor_tensor(out=gw[:, i, :].rearrange("p (h k) -> p h k", h=8), in0=ex[:], in1=bc(sm[:, 1, :], 2, [128, 8, 16]), op=ALU.mult),
                 reads=ex.b + sm.b, writes=[gw.b[i]])

        return {0: [s0], 1: [s1], 2: [s2], 3: [s3], 4: [qmm(0)], 5: [qcp(0, qTs), qmm(1)], 6: [qcp(1, qTs)], 7: [smm(0)],
                8: [qcp(0, sc), smm(1)], 9: [qcp(1, sc)], 10: [k1], 11: [k2], 12: [k3], 13: [k4], 14: [k5], 15: [k6]}

    gctr = [0, 0, 0, 0]
    NPO = int(os.environ.get("NPO", "0"))
    first_gather = [True]
    n_peer = NT if stop_after > 7 else 1

    GS = 4
    NG = 128 // GS

    def stageBC(t):
        i = t % 2
        nxt = make_steps(t + 1) if t + 1 < n_peer else None
        hp = [ps[2 * i], ps[2 * i + 1]]
        jmap = {}

        def tail(g):
            gs = slice(g * GS, (g + 1) * GS)
            gb = g % 16
            p.op("dve", lambda e: e.tensor_tensor(out=wgt[:, gs], in0=actv[:, gs], in1=gw[:, i, gs], op=ALU.mult),
                 reads=[actv.b[gb], gw.b[i]], writes=[wgt.b[gb]])
            for s4 in range(GS):
                s_ = g * GS + s4
                j = jmap[s_]
                d = gctr[2] % ND
                gctr[2] += 1
                p.op("act", lambda e, d=d, s_=s_: e.activation(out=dg[:, d, :], in_=idb[:], func=AF.Copy, scale=wgt[:, s_:s_ + 1]),
                     reads=idb.b + [wgt.b[gb]], writes=[dg.b[d]])
                for half in range(2):
                    p.op("pe", lambda e, d=d, j=j, half=half, s_=s_: e.matmul(ps[4 + half][:, :], lhsT=dg[:, d, :],
                                                                          rhs=ub[:, j, 1024 + half * 512:1024 + (half + 1) * 512],
                                                                          start=(s_ == 0), stop=(s_ == 127)),
                         reads=[dg.b[d], ub.b[j]], writes=ps[4 + half].b)

        for g in range(NG):
            gs = slice(g * GS, (g + 1) * GS)
            gb = g % 16
            for s4 in range(GS):
                s_ = g * GS + s4
                j = gctr[0] % NB
                gctr[0] += 1
                jmap[s_] = j
                rd = [ei.b[i]]
                if first_gather[0]:
                    rd = rd + b_uvb
                    first_gather[0] = False
                p.dma("pool", lambda e, j=j, s_=s_: e.indirect_dma_start(out=ub[:, j, :], out_offset=None, in_=uvb,
                                                                         in_offset=bass.IndirectOffsetOnAxis(ap=ei[:, i, s_:s_ + 1], axis=0)),
                      reads=rd, writes=[ub.b[j]])
                r4 = gctr[1] % 2
                gctr[1] += 1
                p.op("dve", lambda e, j=j, s_=s_, r4=r4: e.scalar_tensor_tensor(
                    out=junkb[:, 2 * r4:2 * r4 + 2, :].rearrange("p a b -> p (a b)"), in0=ub[:, j, 0:1024], scalar=1.0, in1=hnp[i][:, :],
                    op0=ALU.mult, op1=ALU.mult, accum_out=da[:, 0, s_:s_ + 1]),
                     reads=[ub.b[j]] + hp[0].b + hp[1].b, writes=[junkb.b[r4], da.b[r4]])
            p.op("act", lambda e, gs=gs: e.activation(out=actv[:, gs], in_=da[:, 0, gs], func=AF.Gelu), reads=da.b, writes=[actv.b[gb]])
            if g >= 1:
                tail(g - 1)
            if nxt is not None and g % 2 == 1:
                for f_ in nxt[g // 2]:
                    f_()
        tail(NG - 1)
        for half in range(2):
            p.op("dve", lambda e, half=half: e.tensor_tensor(out=fo[:, half * 512:(half + 1) * 512], in0=ps[4 + half][:, :],
                                                             in1=ht[:, i, half * 512:(half + 1) * 512], op=ALU.add),
                 reads=ps[4 + half].b + [ht.b[i]], writes=fo.b)
        p.dma("sp", lambda e: e.dma_start(out=out[t * 128:(t + 1) * 128, :], in_=fo[:]), reads=fo.b, writes=[b_out[t]])

    st0 = make_steps(0)
    for gi_ in range(16):
        for f_ in st0[gi_]:
            f_()
    for t in range(n_peer):
        stageBC(t)
    return finish(nc, p, dbg_out)


def finish(nc, p, dbg_out):
    p.emit()
    return nc, dbg_out


_CONST = {}


def host_consts():
    if _CONST:
        return _CONST
    c = _CONST
    c["identb"] = np.eye(128, dtype=np.float32).astype(ml_dtypes.bfloat16)
    c["identf"] = np.eye(128, dtype=np.float32)
    half = 16
    inv = 10000.0 ** (-np.arange(half, dtype=np.float64) / half)
    ang = np.arange(S, dtype=np.float64)[:, None] * inv[None, :]
    c["rot_cos"] = np.cos(ang).astype(np.float32)
    c["rot_sin"] = np.sin(ang).astype(np.float32)
    L = S
    t = np.linspace(0.0, 1.0, L, dtype=np.float64)[:, None]
    w = 2.0 * math.pi * np.arange(L, dtype=np.float64)[:, None] / L
    f = np.linspace(1e-4, 15, 16, dtype=np.float64)[None, :]
    z = np.concatenate([t, np.cos(f * w), -np.sin(f * w)], axis=-1)
    c["zfeat"] = np.ascontiguousarray(z.T).astype(np.float32)
    min_decay = math.log(1e-2) / 1.5
    max_decay = math.log(1e-2) / 0.3
    deltas = np.abs(np.linspace(min_decay, max_decay, 512, dtype=np.float64))
    decay = np.exp(-t * deltas[None, :])
    c["decay_f"] = decay.astype(np.float32)
    db = np.zeros_like(decay)
    db[:-1] = decay[1:]
    c["decay_b"] = db.astype(np.float32)
    n = np.arange(S, dtype=np.float64)
    th = 2 * np.pi * np.outer(n + 0.5, n + 0.5) / (2 * S)
    for nm, M in (("dftC", np.cos(th)), ("dftS", np.sin(th))):
        T = M.reshape(16, 128, 8, 256).transpose(2, 1, 0, 3)
        c[nm] = np.ascontiguousarray(T).astype(ml_dtypes.bfloat16)
    pc = np.cos(np.pi * (n + 0.5) / (2 * S)) / S
    ps_ = np.sin(np.pi * (n + 0.5) / (2 * S)) / S
    ph = np.stack([pc, ps_], -1).reshape(16, 128, 2).transpose(1, 0, 2)
    c["phase"] = np.ascontiguousarray(ph).astype(np.float32)
    c["iota16"] = np.broadcast_to(np.tile(np.arange(16, dtype=np.float32), 16)[None, :], (128, 256)).copy()
    return c


def make_in_maps(I):
    f = lambda a: np.ascontiguousarray(np.asarray(a, dtype=np.float32))
    c = host_consts()
    shared = dict(c)
    shared["attn_norm"] = f(I["attn_norm"][0][None, :])
    shared["w_in"] = f(I["w_in"][0])
    shared["b_gate"] = f(I["b_gate"][0][None, :])
    shared["q_a_norm"] = f(I["q_a_norm"][0][None, :])
    shared["w_uq"] = f(I["w_uq"][0])
    shared["kv_a_norm"] = f(I["kv_a_norm"][0][None, :])
    shared["w_ukv"] = f(I["w_ukv"][0])
    shared["q_norm"] = f(I["q_norm"][0][None, :])
    shared["k_norm"] = f(I["k_norm"][0][None, :])
    shared["w_o_attn"] = f(I["w_o_attn"][0])
    cw = np.asarray(I["hyena_conv_w"][0])
    shared["conv_w"] = f(cw.T.reshape(12, 128, 3).transpose(1, 0, 2))
    shared["conv_b"] = f(np.asarray(I["hyena_conv_b"][0]).reshape(12, 128).T)
    shared["filt_w1"] = f(I["filt_w1"][0])
    shared["filt_w2"] = f(I["filt_w2"][0])
    shared["filt_w3"] = f(I["filt_w3"][0])
    shared["filt_b"] = f(np.stack([np.asarray(I["filt_b1"][0]), np.asarray(I["filt_b2"][0]), np.asarray(I["filt_b3"][0]),
                                   np.asarray(I["filt_freq"][0])], -1))
    shared["filt_w4a"] = f(np.concatenate([np.asarray(I["filt_w4"][0]), np.asarray(I["filt_b4"][0])[None, :]], 0))
    shared["hy_bias"] = f(I["hyena_bias"][0][None, :])
    shared["w_o_hyena"] = f(I["w_o_hyena"][0])
    shared["w_out"] = f(I["w_out"][0])
    shared["ffn_norm"] = f(I["ffn_norm"][0][None, :])
    shared["peer_w_q"] = f(I["peer_w_q"][0])
    k1 = np.asarray(I["peer_keys1"][0])
    k2 = np.asarray(I["peer_keys2"][0])
    kk = np.stack([k1, k2], 1).reshape(16, 128, 128)
    shared["keysT"] = f(kk.transpose(2, 0, 1))
    shared["expert_u"] = f(I["expert_u"][0])
    shared["expert_v"] = f(I["expert_v"][0])
    xs = np.asarray(I["x"], dtype=np.float32)
    maps = []
    for b in range(8):
        m = dict(shared)
        m["x"] = np.ascontiguousarray(xs[b])
        maps.append(m)
    return maps


_NC_CACHE = {}


def kernel(**inputs):
    if "nc" not in _NC_CACHE:
        _NC_CACHE["nc"] = build_nc()[0]
    nc = _NC_CACHE["nc"]
    in_maps = make_in_maps(inputs)
    res = run_bass_kernel_spmd(nc, in_maps, core_ids=list(range(8)))
    return np.stack([np.asarray(r["out"], dtype=np.float32) for r in res.results], 0)
```

```python
import math
import os
from contextlib import ExitStack

import numpy as np
import ml_dtypes

import concourse.bass as bass
import concourse.mybir as mybir
from concourse.bass_utils import run_bass_kernel_spmd

F32 = mybir.dt.float32
F32R = mybir.dt.float32r
BF16 = mybir.dt.bfloat16
I32 = mybir.dt.int32
U32 = mybir.dt.uint32
AF = mybir.ActivationFunctionType
ALU = mybir.AluOpType
AXX = mybir.AxisListType.X

ENG_NAMES = ("pe", "act", "dve", "pool", "sp")
EPS = 1e-6
S = 2048
NT = 16
PI = math.pi


class Buf:
    __slots__ = ("name", "w", "r", "excl")

    def __init__(self, name):
        self.name = name
        self.w = None
        self.r = []
        self.excl = False


class Prog:
    def __init__(self, nc, n_dma_sems=12):
        self.nc = nc
        self.streams = {e: [] for e in ENG_NAMES}
        self.cnt = {e: 0 for e in ENG_NAMES}
        self.seen = {e: {} for e in ENG_NAMES}
        self.n_dma_sems = n_dma_sems
        self.dma_i = {"sp": 0, "pool": 0, "act": 0}
        self.all_tokens = {}

    def _deps(self, eng, reads, writes):
        toks = []
        for b in reads:
            if b.w is not None:
                toks.append(b.w)
            if b.excl:
                toks.extend(tk for tk in b.r if tk[2] != eng)
        for b in writes:
            if b.w is not None:
                toks.append(b.w)
            toks.extend(b.r)
        need = {}
        for (k, v, e) in toks:
            if e == eng and eng == "pe":
                continue
            if self.seen[eng].get(k, 0) >= v:
                continue
            if need.get(k, 0) < v:
                need[k] = v
        for k, v in need.items():
            self.seen[eng][k] = v
        return list(need.items())

    def _commit(self, tok, reads, writes):
        for b in reads:
            b.r.append(tok)
            if len(b.r) > 64:
                best = {}
                for (k, v, e) in b.r:
                    if k not in best or best[k][1] < v:
                        best[k] = (k, v, e)
                b.r = list(best.values())
        for b in writes:
            b.w = tok
            b.r = []

    def op(self, eng, fn, reads=(), writes=()):
        waits = self._deps(eng, reads, writes)
        self.cnt[eng] += 1
        tok = ("c_" + eng, self.cnt[eng], eng)
        self.all_tokens[tok[0]] = tok[1]
        self.streams[eng].append((waits, fn, (tok[0], 1)))
        self._commit(tok, reads, writes)
        return tok

    def dma(self, q, fn, reads=(), writes=()):
        i = self.dma_i[q]
        self.dma_i[q] += 1
        K = self.n_dma_sems
        key = "d_%s_%d" % (q, i % K)
        val = 16 * (i // K + 1)
        waits = self._deps(q, reads, writes)
        if i >= K and self.seen[q].get(key, 0) < val - 16:
            waits.append((key, val - 16))
            self.seen[q][key] = val - 16
        tok = (key, val, "dma_" + q)
        self.all_tokens[key] = val
        self.streams[q].append((waits, fn, (key, 16)))
        self._commit(tok, reads, writes)
        return tok

    def barrier(self):
        for e in ENG_NAMES:
            waits = []
            for k, v in self.all_tokens.items():
                if k == "c_pe" and e == "pe":
                    continue
                if self.seen[e].get(k, 0) < v:
                    waits.append((k, v))
                    self.seen[e][k] = v
            if waits:
                self.streams[e].append((waits, None, None))

    def emit(self):
        nc = self.nc
        keys = sorted(self.all_tokens.keys())
        sems = {}
        with ExitStack() as st:
            for k in keys:
                sems[k] = st.enter_context(nc.semaphore(k))
            block = st.enter_context(nc.Block())

            def run(ename):
                def body(engine):
                    for (waits, fn, inc) in self.streams[ename]:
                        for (k, v) in waits:
                            engine.wait_ge(sems[k], v)
                        if fn is not None:
                            ins = fn(engine)
                            ins.then_inc(sems[inc[0]], inc[1])
                    if ename == "sp":
                        for k, v in self.all_tokens.items():
                            engine.wait_ge(sems[k], v)
                return body

            block.tensor(run("pe"))
            block.scalar(run("act"))
            block.vector(run("dve"))
            block.gpsimd(run("pool"))
            block.sync(run("sp"))


class Tl:
    def __init__(self, t, nb, name):
        self.t = t
        self.b = [Buf("%s_%d" % (name, i)) for i in range(nb)]

    def __getitem__(self, k):
        return self.t[k]


_DT_SIZE = {F32: 4, BF16: 2, I32: 4, U32: 4}


class Arena:
    def __init__(self, nc, base=16384, limit=16384 + 212000):
        self.nc = nc
        self.off = base
        self.limit = limit
        self.peak = base
        self.n = 0

    def alloc(self, name, shape, dt, nb=1):
        nbytes = int(np.prod(shape[1:])) * _DT_SIZE[dt]
        nbytes = (nbytes + 63) // 64 * 64
        assert self.off + nbytes <= self.limit, ("SBUF overflow", name, self.off, nbytes)
        self.n += 1
        t = self.nc.alloc_sbuf_tensor_at("%s_%d" % (name, self.n), list(shape), dt, offset=self.off)
        self.off += nbytes
        self.peak = max(self.peak, self.off)
        return Tl(t, nb, name)

    def mark(self):
        return self.off

    def release(self, m):
        self.off = m


def bc(ap, axis, shape):
    return ap.unsqueeze(axis).broadcast_to(list(shape))


def build_nc(stop_after=99, dbg=()):
    nc = bass.Bass("TRN2", target_bir_lowering=False)

    def din(name, shape, dt=F32):
        return nc.dram_tensor(name, list(shape), dt, kind="ExternalInput").ap()

    x = din("x", [S, 1024])
    attn_norm = din("attn_norm", [1, 1024])
    w_in = din("w_in", [1024, 4000])
    b_gate = din("b_gate", [1, 2048])
    q_a_norm = din("q_a_norm", [1, 256])
    w_uq = din("w_uq", [256, 768])
    kv_a_norm = din("kv_a_norm", [1, 128])
    w_ukv = din("w_ukv", [128, 1024])
    q_norm = din("q_norm", [1, 96])
    k_norm = din("k_norm", [1, 96])
    w_o_attn = din("w_o_attn", [512, 1024])
    conv_w = din("conv_w", [128, 12, 3])
    conv_b = din("conv_b", [128, 12])
    filt_w1 = din("filt_w1", [33, 64])
    filt_w2 = din("filt_w2", [64, 64])
    filt_w3 = din("filt_w3", [64, 64])
    filt_b = din("filt_b", [64, 4])
    filt_w4a = din("filt_w4a", [65, 1024])
    hy_bias = din("hy_bias", [1, 512])
    w_o_hyena = din("w_o_hyena", [512, 1024])
    w_out = din("w_out", [1024, 1024])
    ffn_norm = din("ffn_norm", [1, 1024])
    peer_w_q = din("peer_w_q", [1024, 2048])
    keysT = din("keysT", [128, 16, 128])
    n_exp = 16384 if stop_after >= 7 else 128
    expert_u = din("expert_u", [n_exp, 1024])
    expert_v = din("expert_v", [n_exp, 1024])
    identb = din("identb", [128, 128], BF16)
    identf = din("identf", [128, 128])
    rot_cos = din("rot_cos", [S, 16])
    rot_sin = din("rot_sin", [S, 16])
    zfeat = din("zfeat", [33, S])
    decay_f = din("decay_f", [S, 512])
    decay_b = din("decay_b", [S, 512])
    dftC = din("dftC", [8, 128, 16, 256], BF16)
    dftS = din("dftS", [8, 128, 16, 256], BF16)
    phase = din("phase", [128, 16, 2])
    iota16 = din("iota16", [128, 256])
    out = nc.dram_tensor("out", [S, 1024], F32, kind="ExternalOutput").ap()
    uvb = nc.dram_tensor("uvb", [n_exp, 2048], BF16).ap()
    b_uvb = []
    dbg_out = {}

    p = Prog(nc)
    ar = Arena(nc)
    hnp = [nc.alloc_psum_tensor("hnp%d" % i, [128, 1024], F32) for i in range(2)]

    class PsView:
        def __init__(self, ap, name):
            self.ap = ap
            self.b = [Buf(name)]

        def __getitem__(self, k):
            return self.ap[k]

    ps = [PsView(hnp[i // 2][:, (i % 2) * 512:(i % 2 + 1) * 512], "ps%d" % i) for i in range(4)]
    ps += [Tl(nc.alloc_psum_tensor("ps%d" % i, [128, 512], F32), 1, "ps%d" % i) for i in range(4, 8)]

    for t_ in ps:
        t_.b[0].excl = True

    def psb(i):
        return ps[i][:].bitcast(BF16)

    b_out = [Buf("out%d" % t) for t in range(NT)]

    def dump(name, tl, shape, dt):
        if name in dbg:
            d = nc.dram_tensor("dbg_" + name, list(shape), dt, kind="ExternalOutput").ap()
            dbg_out[name] = d
            p.dma("sp", lambda e: e.dma_start(out=d, in_=tl[:]), reads=tl.b)

    idb = ar.alloc("idb", [128, 128], BF16)
    idf = ar.alloc("idf", [128, 128], F32)
    p.dma("sp", lambda e: e.dma_start(out=idb[:], in_=identb), writes=idb.b)
    p.dma("sp", lambda e: e.dma_start(out=idf[:], in_=identf), writes=idf.b)
    st = ar.alloc("st", [128, NT, 8], F32)
    junk = ar.alloc("junk", [128, 1024], F32)
    m_persist = ar.mark()

    xT = ar.alloc("xT", [128, 8, S], BF16, nb=NT)
    g1 = ar.alloc("g1", [128, 1024], F32)
    p.dma("sp", lambda e: e.dma_start(out=g1[:], in_=attn_norm.broadcast_to([128, 1024])), writes=g1.b)
    m1 = ar.mark()

    def norm_and_transpose(t, src_ap, src_bufs, gt, dstT_ap, dst_bufs, psi, i, xn_t=None):
        p.op("act", lambda e: e.activation(out=junk[:], in_=src_ap, func=AF.Square, accum_out=st[:, t, 0:1]),
             reads=src_bufs, writes=junk.b + st.b)
        p.op("act", lambda e: e.activation(out=st[:, t, 1:2], in_=st[:, t, 0:1], func=AF.Sqrt, scale=1.0 / 1024, bias=EPS),
             reads=st.b, writes=st.b)
        p.op("dve", lambda e: e.reciprocal(out=st[:, t, 2:3], in_=st[:, t, 1:2]), reads=st.b, writes=st.b)
        p.op("dve", lambda e: e.scalar_tensor_tensor(out=xn_t[:, i, :], in0=src_ap, scalar=st[:, t, 2:3], in1=gt[:],
                                                      op0=ALU.mult, op1=ALU.mult),
             reads=src_bufs + st.b + gt.b, writes=[xn_t.b[i]])
        pt = psb(psi)
        for k in range(8):
            p.op("pe", lambda e, k=k: e.transpose(out=pt[:, k * 128:(k + 1) * 128], in_=xn_t[:, i, k * 128:(k + 1) * 128], identity=idb[:]),
                 reads=[xn_t.b[i]] + idb.b, writes=ps[psi].b)
        p.op("act", lambda e: e.copy(out=dstT_ap, in_=pt.rearrange("p (k n) -> p k n", k=8)),
             reads=ps[psi].b, writes=dst_bufs)

    qT = ar.alloc("qT", [128, 8, S], BF16, nb=1)
    kT = ar.alloc("kT", [128, 8, S], BF16, nb=1)
    v_sb = ar.alloc("v_sb", [128, NT, 8, 68], BF16, nb=1)
    m2 = ar.mark()
    w_a = ar.alloc("w_a", [128, 8, 416], BF16)
    p.dma("pool", lambda e: e.dma_start(out=w_a[:], in_=w_in[:, 0:416].rearrange("(k p) n -> p k n", p=128)), writes=w_a.b)
    w_uq_sb = ar.alloc("w_uq", [128, 2, 768], BF16)
    p.dma("pool", lambda e: e.dma_start(out=w_uq_sb[:], in_=w_uq.rearrange("(k p) n -> p k n", p=128)), writes=w_uq_sb.b)
    w_ukv_sb = ar.alloc("w_ukv", [128, 1024], BF16)
    p.dma("pool", lambda e: e.dma_start(out=w_ukv_sb[:], in_=w_ukv), writes=w_ukv_sb.b)
    CRB = 1024
    for rb in range(n_exp // CRB if n_exp >= CRB else 0):
        for k_, src_ in enumerate((expert_u, expert_v)):
            bb_ = Buf("uvb%d_%d" % (rb, k_))
            b_uvb.append(bb_)
            p.dma("pool", lambda e, rb=rb, k_=k_, src_=src_: e.dma_start(out=uvb[rb * CRB:(rb + 1) * CRB, k_ * 1024:(k_ + 1) * 1024],
                                                                   in_=src_[rb * CRB:(rb + 1) * CRB, :]), writes=[bb_])
    gq = ar.alloc("gq", [128, 256], F32)
    gkv = ar.alloc("gkv", [128, 128], F32)
    n96 = ar.alloc("n96", [128, 2, 96], F32)
    p.dma("sp", lambda e: e.dma_start(out=gq[:], in_=q_a_norm.broadcast_to([128, 256])), writes=gq.b)
    p.dma("sp", lambda e: e.dma_start(out=gkv[:], in_=kv_a_norm.broadcast_to([128, 128])), writes=gkv.b)
    p.dma("sp", lambda e: e.dma_start(out=n96[:, 0, :], in_=q_norm.broadcast_to([128, 96])), writes=n96.b)
    p.dma("sp", lambda e: e.dma_start(out=n96[:, 1, :], in_=k_norm.broadcast_to([128, 96])), writes=n96.b)
    g96 = ar.alloc("g96", [128, 2, 8, 96], F32)
    p.op("dve", lambda e: e.tensor_scalar(out=g96[:, 0, :, :], in0=bc(n96[:, 0, :], 1, [128, 8, 96]), scalar1=96.0 ** -0.5,
                                          scalar2=None, op0=ALU.mult), reads=n96.b, writes=g96.b)
    p.op("dve", lambda e: e.tensor_copy(out=g96[:, 1, :, :], in_=bc(n96[:, 1, :], 1, [128, 8, 96])), reads=n96.b, writes=g96.b)
    rcos = ar.alloc("rcos", [128, NT, 16], F32)
    rsin = ar.alloc("rsin", [128, NT, 16], F32)
    p.dma("sp", lambda e: e.dma_start(out=rcos[:], in_=rot_cos.rearrange("(n p) d -> p n d", p=128)), writes=rcos.b)
    p.dma("sp", lambda e: e.dma_start(out=rsin[:], in_=rot_sin.rearrange("(n p) d -> p n d", p=128)), writes=rsin.b)
    p.op("dve", lambda e: e.memset(v_sb[:, :, :, 64:65], 1.0), writes=v_sb.b)
    xt = ar.alloc("xt", [128, 2, 1024], F32, nb=2)
    xn = ar.alloc("xn", [128, 2, 1024], BF16, nb=2)
    cqn = ar.alloc("cqn", [128, 2, 384], BF16, nb=2)
    cT = ar.alloc("cT", [128, 2, 3, 128], BF16, nb=2)
    qs = ar.alloc("qs", [128, 2, 2, 768], F32, nb=2)
    sq = ar.alloc("sq", [128, 1536], F32)
    hst = ar.alloc("hst", [128, 4, 16], F32)
    qn = ar.alloc("qn", [128, 2, 8, 96], F32)
    rt = ar.alloc("rt", [128, 4, 16, 16], F32)
    qr = ar.alloc("qr", [128, 2, 8, 96], BF16)
    kpe = ar.alloc("kpe", [128, 2, 32], F32, nb=2)


    def headnorm(t):
        se = t % 2
        src = qs[:, se, :, :].rearrange("p w f -> p (w f)")
        src3 = qs[:, se, :, :].rearrange("p w (h d) -> p (w h) d", h=8)
        bsrc = [qs.b[se]]
        qn3 = qn[:].rearrange("p w h d -> p (w h) d")
        g3 = g96[:].rearrange("p w h d -> p (w h) d")
        qr3 = qr[:].rearrange("p w h d -> p (w h) d")
        p.op("act", lambda e: e.activation(out=sq[:], in_=src, func=AF.Square), reads=bsrc, writes=sq.b)
        p.op("dve", lambda e: e.tensor_reduce(out=hst[:, 0, :], in_=sq[:].rearrange("p (h d) -> p h d", h=16), axis=AXX, op=ALU.add),
             reads=sq.b, writes=hst.b)
        p.op("act", lambda e: e.activation(out=hst[:, 1, :], in_=hst[:, 0, :], func=AF.Sqrt, scale=1.0 / 96, bias=EPS),
             reads=hst.b, writes=hst.b)
        p.op("dve", lambda e: e.reciprocal(out=hst[:, 2, :], in_=hst[:, 1, :]), reads=hst.b, writes=hst.b)
        p.op("dve", lambda e: e.tensor_tensor(out=qn3, in0=src3, in1=bc(hst[:, 2, :], 2, [128, 16, 96]), op=ALU.mult),
             reads=bsrc + hst.b, writes=qn.b)
        p.op("dve", lambda e: e.tensor_tensor(out=qn3, in0=qn3, in1=g3, op=ALU.mult),
             reads=qn.b + g96.b, writes=qn.b)
        x1 = qn3[:, :, 64:80]
        x2 = qn3[:, :, 80:96]
        cosb = bc(rcos[:, t, :], 1, [128, 16, 16])
        sinb = bc(rsin[:, t, :], 1, [128, 16, 16])
        p.op("dve", lambda e: e.tensor_tensor(out=rt[:, 0, :, :], in0=x1, in1=cosb, op=ALU.mult), reads=qn.b + rcos.b, writes=rt.b)
        p.op("dve", lambda e: e.tensor_tensor(out=rt[:, 1, :, :], in0=x2, in1=sinb, op=ALU.mult), reads=qn.b + rsin.b, writes=rt.b)
        p.op("dve", lambda e: e.tensor_tensor(out=rt[:, 2, :, :], in0=x2, in1=cosb, op=ALU.mult), reads=qn.b + rcos.b, writes=rt.b)
        p.op("dve", lambda e: e.tensor_tensor(out=rt[:, 3, :, :], in0=x1, in1=sinb, op=ALU.mult), reads=qn.b + rsin.b, writes=rt.b)
        p.op("dve", lambda e: e.tensor_tensor(out=qr3[:, :, 64:80], in0=rt[:, 0, :, :], in1=rt[:, 1, :, :], op=ALU.subtract),
             reads=rt.b, writes=qr.b)
        p.op("dve", lambda e: e.tensor_tensor(out=qr3[:, :, 80:96], in0=rt[:, 2, :, :], in1=rt[:, 3, :, :], op=ALU.add),
             reads=rt.b, writes=qr.b)
        p.op("act", lambda e: e.copy(out=qr3[:, :, 0:64], in_=qn3[:, :, 0:64]), reads=qn.b, writes=qr.b)
        for which, dstT in ((0, qT), (1, kT)):
            pt = psb(which)
            for h in range(8):
                p.op("pe", lambda e, h=h, which=which, pt=pt: e.transpose(out=pt[0:96, h * 128:(h + 1) * 128], in_=qr[:, which, h, :], identity=idb[:]),
                     reads=qr.b + idb.b, writes=ps[which].b)
            p.op("act", lambda e, pt=pt, dstT=dstT: e.copy(out=dstT[0:96, :, t * 128:(t + 1) * 128], in_=pt[0:96, :].rearrange("p (h n) -> p h n", h=8)),
                 reads=ps[which].b, writes=dstT.b)

    def p1a(t):
        i = t % 2
        p.dma("sp", lambda e: e.dma_start(out=xt[:, i, :], in_=x[t * 128:(t + 1) * 128, :]), writes=[xt.b[i]])
        norm_and_transpose(t, xt[:, i, :], [xt.b[i]], g1, xT[:, :, t * 128:(t + 1) * 128], [xT.b[t]], 3, i, xn_t=xn)

    def p1b_front(t):
        se = t % 2
        pa = ps[2]
        for k in range(8):
            p.op("pe", lambda e, k=k: e.matmul(pa[:, 0:416], lhsT=xT[:, k, t * 128:(t + 1) * 128], rhs=w_a[:, k, :],
                                              start=(k == 0), stop=(k == 7)),
                 reads=[xT.b[t]] + w_a.b, writes=pa.b)
        p.op("act", lambda e: e.activation(out=junk[:, 0:256], in_=pa[:, 0:256], func=AF.Square, accum_out=st[:, t, 3:4]),
             reads=pa.b, writes=junk.b + st.b)
        p.op("act", lambda e: e.activation(out=junk[:, 256:384], in_=pa[:, 256:384], func=AF.Square, accum_out=st[:, t, 4:5]),
             reads=pa.b, writes=junk.b + st.b)
        p.op("act", lambda e: e.activation(out=st[:, t, 5:6], in_=st[:, t, 3:4], func=AF.Sqrt, scale=1.0 / 256, bias=EPS),
             reads=st.b, writes=st.b)
        p.op("act", lambda e: e.activation(out=st[:, t, 6:7], in_=st[:, t, 4:5], func=AF.Sqrt, scale=1.0 / 128, bias=EPS),
             reads=st.b, writes=st.b)
        p.op("act", lambda e: e.copy(out=kpe[:, se, :], in_=pa[:, 384:416]), reads=pa.b, writes=[kpe.b[se]])
        p.op("dve", lambda e: e.reciprocal(out=st[:, t, 3:5], in_=st[:, t, 5:7]), reads=st.b, writes=st.b)
        p.op("dve", lambda e: e.scalar_tensor_tensor(out=cqn[:, se, 0:256], in0=pa[:, 0:256], scalar=st[:, t, 3:4], in1=gq[:],
                                                      op0=ALU.mult, op1=ALU.mult),
             reads=pa.b + st.b + gq.b, writes=[cqn.b[se]])
        p.op("dve", lambda e: e.scalar_tensor_tensor(out=cqn[:, se, 256:384], in0=pa[:, 256:384], scalar=st[:, t, 4:5], in1=gkv[:],
                                                      op0=ALU.mult, op1=ALU.mult),
             reads=pa.b + st.b + gkv.b, writes=[cqn.b[se]])
        pt5 = psb(5)
        for k in range(3):
            p.op("pe", lambda e, k=k: e.transpose(out=pt5[:, 512 + k * 128:512 + (k + 1) * 128], in_=cqn[:, se, k * 128:(k + 1) * 128], identity=idb[:]),
                 reads=[cqn.b[se]] + idb.b, writes=ps[5].b)
        p.op("act", lambda e: e.copy(out=cT[:, se, :, :], in_=pt5[:, 512:896].rearrange("p (k n) -> p k n", k=3)), reads=ps[5].b, writes=[cT.b[se]])
        for k in range(2):
            p.op("pe", lambda e, k=k: e.matmul(ps[4][:, 0:512], lhsT=cT[:, se, k, :], rhs=w_uq_sb[:, k, 0:512], start=(k == 0), stop=(k == 1)),
                 reads=[cT.b[se]] + w_uq_sb.b, writes=ps[4].b)
        for k in range(2):
            p.op("pe", lambda e, k=k: e.matmul(ps[5][:, 0:256], lhsT=cT[:, se, k, :], rhs=w_uq_sb[:, k, 512:768], start=(k == 0), stop=(k == 1)),
                 reads=[cT.b[se]] + w_uq_sb.b, writes=ps[5].b)
        p.op("pe", lambda e: e.matmul(ps[6][:, 0:512], lhsT=cT[:, se, 2, :], rhs=w_ukv_sb[:, 0:512], start=True, stop=True),
             reads=[cT.b[se]] + w_ukv_sb.b, writes=ps[6].b)
        p.op("pe", lambda e: e.matmul(ps[7][:, 0:512], lhsT=cT[:, se, 2, :], rhs=w_ukv_sb[:, 512:1024], start=True, stop=True),
             reads=[cT.b[se]] + w_ukv_sb.b, writes=ps[7].b)
        p.op("act", lambda e: e.copy(out=qs[:, se, 0, 0:512], in_=ps[4][:, 0:512]), reads=ps[4].b, writes=[qs.b[se]])
        p.op("act", lambda e: e.copy(out=qs[:, se, 0, 512:768], in_=ps[5][:, 0:256]), reads=ps[5].b, writes=[qs.b[se]])
        k3 = qs[:, se, 1, :].rearrange("p (h d) -> p h d", h=8)
        for hb_, pk in ((0, 6), (1, 7)):
            kvv = ps[pk][:, 0:512].rearrange("p (h d) -> p h d", h=4)
            p.op("act", lambda e, hb_=hb_, kvv=kvv: e.copy(out=k3[:, hb_ * 4:(hb_ + 1) * 4, 0:64], in_=kvv[:, :, 0:64]),
                 reads=ps[pk].b, writes=[qs.b[se]])
            p.op("act", lambda e, hb_=hb_, kvv=kvv: e.copy(out=v_sb[:, t, hb_ * 4:(hb_ + 1) * 4, 0:64], in_=kvv[:, :, 64:128]),
                 reads=ps[pk].b, writes=v_sb.b)
        p.op("dve", lambda e: e.tensor_copy(out=k3[:, :, 64:96], in_=bc(kpe[:, se, :], 1, [128, 8, 32])), reads=[kpe.b[se]], writes=[qs.b[se]])

    def record(fn, *args):
        rec = []
        orig_op, orig_dma = p.op, p.dma
        p.op = lambda *a, **k: rec.append((orig_op, a, k))
        p.dma = lambda *a, **k: rec.append((orig_dma, a, k))
        try:
            fn(*args)
        finally:
            p.op, p.dma = orig_op, orig_dma
        return rec

    def interleave(*recs):
        items = []
        for ci, rec in enumerate(recs):
            n = len(rec)
            for i_, it in enumerate(rec):
                items.append(((i_ + 0.5) / n, ci, i_, it))
        items.sort(key=lambda z: (z[0], z[1], z[2]))
        for (_, _, _, (f_, a_, k_)) in items:
            f_(*a_, **k_)

    p1a(0)
    p1a(1)
    p1b_front(0)
    for t in range(NT):
        recs = [record(headnorm, t)]
        if t + 1 < NT:
            recs.append(record(p1b_front, t + 1))
        if t + 2 < NT:
            recs.append(record(p1a, t + 2))
        interleave(*recs)
    dump("qT", qT, [128, 8, S], BF16)
    dump("kT", kT, [128, 8, S], BF16)
    dump("v_sb", v_sb, [128, NT, 8, 68], BF16)
    if stop_after <= 2:
        return finish(nc, p, dbg_out)

    p.barrier()
    ar.release(m2)
    zT = ar.alloc("zT", [128, NT, 512], BF16)
    x0s = ar.alloc("x0s", [128, 4, S], BF16)
    m3 = ar.mark()
    cw = ar.alloc("cw", [128, 12, 3], F32)
    cbv = ar.alloc("cbv", [128, 12], F32)
    p.dma("sp", lambda e: e.dma_start(out=cw[:], in_=conv_w), writes=cw.b)
    p.dma("sp", lambda e: e.dma_start(out=cbv[:], in_=conv_b), writes=cbv.b)
    wh = ar.alloc("wh", [128, 2, 8, 128], BF16, nb=2)
    ubuf = ar.alloc("ubuf", [128, S + 2], F32)
    cbA = ar.alloc("cbA", [128, S], F32)
    cbB = ar.alloc("cbB", [128, S], F32)
    zj = ar.alloc("zj", [128, S], BF16)
    p.op("dve", lambda e: e.memset(ubuf[:, 0:1], 0.0), writes=ubuf.b)
    p.op("dve", lambda e: e.memset(ubuf[:, S + 1:S + 2], 0.0), writes=ubuf.b)
    wi = 0
    deferred = []
    for j in range(4):
        for part, dst in ((1, cbA), (2, cbB), (0, None)):
            col0 = 416 + part * 512 + j * 128
            ci = part * 4 + j
            i = wi % 2
            wi += 1
            p.dma("pool", lambda e, i=i, col0=col0: e.dma_start(out=wh[:, i, :, :], in_=w_in[:, col0:col0 + 128].rearrange("(k p) n -> p k n", p=128)),
                  writes=[wh.b[i]])
            for tb in range(4):
                for k in range(8):
                    p.op("pe", lambda e, i=i, k=k, tb=tb: e.matmul(ps[tb][:, :], lhsT=wh[:, i, k, :], rhs=xT[:, k, tb * 512:(tb + 1) * 512],
                                                               start=(k == 0), stop=(k == 7)),
                         reads=[wh.b[i]] + xT.b[tb * 4:(tb + 1) * 4], writes=ps[tb].b)
                p.op("act", lambda e, tb=tb: e.copy(out=ubuf[:, 1 + tb * 512:1 + (tb + 1) * 512], in_=ps[tb][:, :]), reads=ps[tb].b, writes=ubuf.b)
            while deferred:
                deferred.pop(0)()
            if dst is None:
                dst = cbA
            p.op("dve", lambda e, dst=dst, ci=ci: e.tensor_scalar(out=dst[:], in0=ubuf[:, 1:S + 1], scalar1=cw[:, ci, 1:2], scalar2=cbv[:, ci:ci + 1],
                                                                  op0=ALU.mult, op1=ALU.add),
                 reads=ubuf.b + cw.b + cbv.b, writes=dst.b)
            p.op("dve", lambda e, dst=dst, ci=ci: e.scalar_tensor_tensor(out=dst[:], in0=ubuf[:, 0:S], scalar=cw[:, ci, 0:1], in1=dst[:],
                                                                         op0=ALU.mult, op1=ALU.add),
                 reads=ubuf.b + cw.b + dst.b, writes=dst.b)
            p.op("dve", lambda e, dst=dst, ci=ci: e.scalar_tensor_tensor(out=dst[:], in0=ubuf[:, 2:S + 2], scalar=cw[:, ci, 2:3], in1=dst[:],
                                                                         op0=ALU.mult, op1=ALU.add),
                 reads=ubuf.b + cw.b + dst.b, writes=dst.b)
            if part == 2:
                p.op("dve", lambda e: e.tensor_tensor(out=zj[:], in0=cbA[:], in1=cbB[:], op=ALU.mult), reads=cbA.b + cbB.b, writes=zj.b)

                def z_transposes(j=j):
                  for half in range(2):
                    pt = psb(4 + half)
                    for q_ in range(8):
                        tcn = half * 8 + q_
                        p.op("pe", lambda e, q_=q_, tcn=tcn, pt=pt: e.transpose(out=pt[:, q_ * 128:(q_ + 1) * 128], in_=zj[:, tcn * 128:(tcn + 1) * 128], identity=idb[:]),
                             reads=zj.b + idb.b, writes=ps[4 + half].b)
                    p.op("act", lambda e, half=half, pt=pt, j=j: e.copy(out=zT[:, half * 8:(half + 1) * 8, j * 128:(j + 1) * 128],
                                                                    in_=pt.rearrange("p (k n) -> p k n", k=8)),
                         reads=ps[4 + half].b, writes=zT.b)
                deferred.append(z_transposes)
            if part == 0:
                p.op("act", lambda e, j=j: e.copy(out=x0s[:, j, :], in_=cbA[:]), reads=cbA.b, writes=x0s.b)
    dump("zT", zT, [128, NT, 512], BF16)
    dump("x0s", x0s, [128, 4, S], BF16)
    if stop_after <= 3:
        return finish(nc, p, dbg_out)

    p.barrier()
    ar.release(m3)
    attn_o = ar.alloc("attn_o", [128, NT, 512], BF16)
    m4 = ar.mark()
    PT = ar.alloc("PT", [128, 2, NT, 512], BF16, nb=2 * NT)
    rc = ar.alloc("rc", [128, 4], F32, nb=4)
    def attn_S(n_):
        h, qg = divmod(n_, 4)
        bi = n_ % 2
        for kc in range(NT):
            sp_ = ps[kc % 2]
            p.op("pe", lambda e, kc=kc, sp_=sp_: e.matmul(sp_[:, :], lhsT=kT[0:96, h, kc * 128:(kc + 1) * 128],
                                                      rhs=qT[0:96, h, qg * 512:(qg + 1) * 512], start=True, stop=True),
                 reads=kT.b + qT.b, writes=sp_.b)
            p.op("act", lambda e, kc=kc, sp_=sp_: e.activation(out=PT[:, bi, kc, :], in_=sp_[:, :], func=AF.Exp),
                 reads=sp_.b, writes=[PT.b[bi * NT + kc]])

    def attn_PV(n_):
        h, qg = divmod(n_, 4)
        bi = n_ % 2
        for qt in range(4):
            oi = n_ * 4 + qt
            po = ps[2 + oi % 2]
            ri = oi % 4
            tile_i = qg * 4 + qt
            for kc in range(NT):
                p.op("pe", lambda e, kc=kc, qt=qt, po=po: e.matmul(po[:, 0:65], lhsT=PT[:, bi, kc, qt * 128:(qt + 1) * 128],
                                                               rhs=v_sb[:, kc, h, 0:65], start=(kc == 0), stop=(kc == NT - 1)),
                     reads=[PT.b[bi * NT + kc]] + v_sb.b, writes=po.b)
            p.op("dve", lambda e, ri=ri, po=po: e.reciprocal(out=rc[:, ri:ri + 1], in_=po[:, 64:65]), reads=po.b, writes=[rc.b[ri]])
            p.op("dve", lambda e, ri=ri, po=po, tile_i=tile_i: e.tensor_scalar(out=attn_o[:, tile_i, h * 64:(h + 1) * 64], in0=po[:, 0:64],
                                                                          scalar1=rc[:, ri:ri + 1], scalar2=None, op0=ALU.mult),
                 reads=po.b + [rc.b[ri]], writes=attn_o.b)

    attn_S(0)
    for n_ in range(32):
        recs = [record(attn_PV, n_)]
        if n_ + 1 < 32:
            recs.append(record(attn_S, n_ + 1))
        interleave(*recs)
    dump("attn_o", attn_o, [128, NT, 512], BF16)
    if stop_after <= 4:
        return finish(nc, p, dbg_out)

    p.barrier()
    ar.release(m4)
    lo = Arena(nc, base=m_persist, limit=m2)
    lo.n = 1000
    ab = lo.alloc("ab", [128, 2, NT, 512], BF16)
    ml = lo.mark()
    zf = lo.alloc("zf", [64, S], F32)
    hA = lo.alloc("hA", [128, S + 1], F32)
    hB = lo.alloc("hB", [128, S + 1], F32)
    pre = lo.alloc("pre", [64, S], F32)
    kf = lo.alloc("kf", [64, S], F32)
    ki = lo.alloc("ki", [64, S], I32)
    w1s = lo.alloc("w1s", [64, 64], F32)
    w2s = lo.alloc("w2s", [64, 64], F32)
    w3s = lo.alloc("w3s", [64, 64], F32)
    fbs = lo.alloc("fbs", [64, 8], F32)
    w4s = lo.alloc("w4s", [128, 1024], F32)
    p.dma("sp", lambda e: e.dma_start(out=zf[0:33, :], in_=zfeat), writes=zf.b)
    p.dma("sp", lambda e: e.dma_start(out=w1s[0:33, :], in_=filt_w1), writes=w1s.b)
    p.dma("sp", lambda e: e.dma_start(out=w2s[:], in_=filt_w2), writes=w2s.b)
    p.dma("sp", lambda e: e.dma_start(out=w3s[:], in_=filt_w3), writes=w3s.b)
    p.dma("sp", lambda e: e.dma_start(out=fbs[:, 0:4], in_=filt_b), writes=fbs.b)
    p.dma("sp", lambda e: e.dma_start(out=w4s[0:65, :], in_=filt_w4a), writes=w4s.b)
    p.op("dve", lambda e: e.tensor_scalar(out=fbs[:, 4:7], in0=fbs[:, 0:3], scalar1=fbs[:, 3:4], scalar2=None, op0=ALU.mult),
         reads=fbs.b, writes=fbs.b)
    for hh in (hA, hB):
        p.op("dve", lambda e, hh=hh: e.memset(hh[0:64, S:S + 1], 0.0), writes=hh.b)
        p.op("dve", lambda e, hh=hh: e.memset(hh[64:65, :], 1.0), writes=hh.b)

    def sin_layer(l, wt, kdim, src, dst):
        for tb in range(4):
            p.op("pe", lambda e, tb=tb: e.matmul(ps[tb][0:64, :], lhsT=wt[0:kdim, :], rhs=src[0:kdim, tb * 512:(tb + 1) * 512], start=True, stop=True),
                 reads=wt.b + src.b, writes=ps[tb].b)
            p.op("dve", lambda e, tb=tb: e.tensor_scalar(out=pre[:, tb * 512:(tb + 1) * 512], in0=ps[tb][0:64, :], scalar1=fbs[:, 3:4],
                                                         scalar2=fbs[:, 4 + l:5 + l], op0=ALU.mult, op1=ALU.add),
                 reads=ps[tb].b + fbs.b, writes=pre.b)
        p.op("dve", lambda e: e.tensor_scalar(out=ki[:], in0=pre[:], scalar1=1.0 / (2 * PI), scalar2=8.5, op0=ALU.mult, op1=ALU.add),
             reads=pre.b, writes=ki.b)
        p.op("dve", lambda e: e.tensor_copy(out=kf[:], in_=ki[:]), reads=ki.b, writes=kf.b)
        p.op("dve", lambda e: e.tensor_scalar(out=kf[:], in0=kf[:], scalar1=-8.0, scalar2=-2 * PI, op0=ALU.add, op1=ALU.mult),
             reads=kf.b, writes=kf.b)
        p.op("dve", lambda e: e.tensor_tensor(out=pre[:], in0=pre[:], in1=kf[:], op=ALU.add), reads=pre.b + kf.b, writes=pre.b)
        p.op("dve", lambda e: e.tensor_scalar(out=kf[:], in0=pre[:], scalar1=-PI, scalar2=None, op0=ALU.is_lt), reads=pre.b, writes=kf.b)
        p.op("dve", lambda e: e.scalar_tensor_tensor(out=pre[:], in0=kf[:], scalar=2 * PI, in1=pre[:], op0=ALU.mult, op1=ALU.add),
             reads=pre.b + kf.b, writes=pre.b)
        p.op("dve", lambda e: e.tensor_scalar(out=kf[:], in0=pre[:], scalar1=PI, scalar2=None, op0=ALU.is_gt), reads=pre.b, writes=kf.b)
        p.op("dve", lambda e: e.scalar_tensor_tensor(out=pre[:], in0=kf[:], scalar=-2 * PI, in1=pre[:], op0=ALU.mult, op1=ALU.add),
             reads=pre.b + kf.b, writes=pre.b)
        p.op("dve", lambda e: e.tensor_scalar(out=pre[:], in0=pre[:], scalar1=-3.141592, scalar2=3.141592, op0=ALU.max, op1=ALU.min),
             reads=pre.b, writes=pre.b)
        p.op("act", lambda e: e.activation(out=dst[0:64, 0:S], in_=pre[:], func=AF.Sin), reads=pre.b, writes=dst.b)

    sin_layer(0, w1s, 33, zf, hA)
    sin_layer(1, w2s, 64, hA, hB)
    sin_layer(2, w3s, 64, hB, hA)
    dump("h3", hA, [128, S + 1], F32)
    dec = lo.alloc("dec", [128, 2, 2, 512], F32, nb=2)
    t12 = lo.alloc("t12", [128, 2, 2, 512], F32, nb=2)
    brow = lo.alloc("brow", [128, 512], F32)
    p.dma("sp", lambda e: e.dma_start(out=brow[0:1, :], in_=hy_bias), writes=brow.b)
    for n_ in range(NT):
        i = n_ % 2
        pf, pb_ = ps[4 + 2 * i], ps[5 + 2 * i]
        p.op("pe", lambda e, n_=n_, pf=pf: e.matmul(pf[:, :], lhsT=hA[0:65, n_ * 128:(n_ + 1) * 128], rhs=w4s[0:65, 0:512], start=True, stop=True),
             reads=hA.b + w4s.b, writes=pf.b)
        p.op("pe", lambda e, n_=n_, pb_=pb_: e.matmul(pb_[:, :], lhsT=hA[0:65, n_ * 128 + 1:(n_ + 1) * 128 + 1], rhs=w4s[0:65, 512:1024], start=True, stop=True),
             reads=hA.b + w4s.b, writes=pb_.b)
        p.dma("sp", lambda e, n_=n_, i=i: e.dma_start(out=dec[:, i, 0, :], in_=decay_f[n_ * 128:(n_ + 1) * 128, :]), writes=[dec.b[i]])
        p.dma("sp", lambda e, n_=n_, i=i: e.dma_start(out=dec[:, i, 1, :], in_=decay_b[n_ * 128:(n_ + 1) * 128, :]), writes=[dec.b[i]])
        p.op("dve", lambda e, i=i, pf=pf: e.tensor_tensor(out=t12[:, i, 0, :], in0=pf[:, :], in1=dec[:, i, 0, :], op=ALU.mult),
             reads=pf.b + [dec.b[i]], writes=[t12.b[i]])
        p.op("dve", lambda e, i=i, pb_=pb_: e.tensor_tensor(out=t12[:, i, 1, :], in0=pb_[:, :], in1=dec[:, i, 1, :], op=ALU.mult),
             reads=pb_.b + [dec.b[i]], writes=[t12.b[i]])
        if n_ == 0:
            p.op("dve", lambda e, i=i: e.tensor_tensor(out=t12[0:1, i, 0, :], in0=t12[0:1, i, 0, :], in1=brow[0:1, :], op=ALU.add),
                 reads=[t12.b[i]] + brow.b, writes=[t12.b[i]])
        p.op("dve", lambda e, i=i, n_=n_: e.tensor_tensor(out=ab[:, 0, n_, :], in0=t12[:, i, 0, :], in1=t12[:, i, 1, :], op=ALU.add),
             reads=[t12.b[i]], writes=ab.b)
        p.op("dve", lambda e, i=i, n_=n_: e.tensor_tensor(out=ab[:, 1, n_, :], in0=t12[:, i, 0, :], in1=t12[:, i, 1, :], op=ALU.subtract),
             reads=[t12.b[i]], writes=ab.b)
    dump("ab", ab, [128, 2, NT, 512], BF16)
    p.barrier()
    lo.release(ml)
    Yr = lo.alloc("Yr", [128, 2, NT, 512], BF16)
    tabs = lo.alloc("tabs", [128, 2, 2, NT, 256], BF16, nb=2)
    phs = lo.alloc("phs", [128, NT, 2], F32)
    p.dma("sp", lambda e: e.dma_start(out=phs[:], in_=phase), writes=phs.b)
    kk = ar.alloc("kk", [128, 2, 4, 512], F32, nb=2)
    yy = ar.alloc("yy", [128, 2, 4, 512], F32, nb=2)
    for cb in range(8):
        ti = cb % 2
        p.dma("sp", lambda e, cb=cb, ti=ti: e.dma_start(out=tabs[:, ti, 0, :, :], in_=dftC[cb]), writes=[tabs.b[ti]])
        p.dma("sp", lambda e, cb=cb, ti=ti: e.dma_start(out=tabs[:, ti, 1, :, :], in_=dftS[cb]), writes=[tabs.b[ti]])
        for half in range(2):
            fc = cb * 2 + half
            pi_ = fc % 2
            pz = [ps[4 * pi_ + q_] for q_ in range(4)]
            for q_, (cs_, src, si) in enumerate(((0, zT, None), (1, zT, None), (0, ab, 0), (1, ab, 1))):
                for sc in range(NT):
                    rhs = zT[:, sc, :] if si is None else ab[:, si, sc, :]
                    p.op("pe", lambda e, q_=q_, cs_=cs_, sc=sc, rhs=rhs, ti=ti, half=half, pz=pz: e.matmul(
                        pz[q_][:, :], lhsT=tabs[:, ti, cs_, sc, half * 128:(half + 1) * 128], rhs=rhs, start=(sc == 0), stop=(sc == NT - 1)),
                         reads=[tabs.b[ti]] + src.b, writes=pz[q_].b)
            Zc, Zs, Kc, Ks = pz
            pcs = phs[:, fc, 0:1]
            pss = phs[:, fc, 1:2]
            bk = [kk.b[pi_]]
            by = [yy.b[pi_]]
            p.op("dve", lambda e, Kc=Kc, pcs=pcs, pi_=pi_: e.tensor_scalar(out=kk[:, pi_, 0, :], in0=Kc[:, :], scalar1=pcs, scalar2=None, op0=ALU.mult),
                 reads=Kc.b + phs.b, writes=bk)
            p.op("dve", lambda e, Ks=Ks, pss=pss, pi_=pi_: e.scalar_tensor_tensor(out=kk[:, pi_, 1, :], in0=Ks[:, :], scalar=pss, in1=kk[:, pi_, 0, :],
                                                                             op0=ALU.mult, op1=ALU.add),
                 reads=Ks.b + phs.b + bk, writes=bk)
            p.op("dve", lambda e, Ks=Ks, pcs=pcs, pi_=pi_: e.tensor_scalar(out=kk[:, pi_, 2, :], in0=Ks[:, :], scalar1=pcs, scalar2=None, op0=ALU.mult),
                 reads=Ks.b + phs.b, writes=bk)
            p.op("dve", lambda e, Kc=Kc, pss=pss, pi_=pi_: e.scalar_tensor_tensor(out=kk[:, pi_, 3, :], in0=Kc[:, :], scalar=pss, in1=kk[:, pi_, 2, :],
                                                                             op0=ALU.mult, op1=ALU.subtract),
                 reads=Kc.b + phs.b + bk, writes=bk)
            p.op("dve", lambda e, Zc=Zc, pi_=pi_: e.tensor_tensor(out=yy[:, pi_, 0, :], in0=Zc[:, :], in1=kk[:, pi_, 1, :], op=ALU.mult),
                 reads=Zc.b + bk, writes=by)
            p.op("dve", lambda e, Zs=Zs, pi_=pi_: e.tensor_tensor(out=yy[:, pi_, 1, :], in0=Zs[:, :], in1=kk[:, pi_, 3, :], op=ALU.mult),
                 reads=Zs.b + bk, writes=by)
            p.op("dve", lambda e, Zs=Zs, pi_=pi_: e.tensor_tensor(out=yy[:, pi_, 2, :], in0=Zs[:, :], in1=kk[:, pi_, 1, :], op=ALU.mult),
                 reads=Zs.b + bk, writes=by)
            p.op("dve", lambda e, Zc=Zc, pi_=pi_: e.tensor_tensor(out=yy[:, pi_, 3, :], in0=Zc[:, :], in1=kk[:, pi_, 3, :], op=ALU.mult),
                 reads=Zc.b + bk, writes=by)
            p.op("dve", lambda e, pi_=pi_, fc=fc: e.tensor_tensor(out=Yr[:, 0, fc, :], in0=yy[:, pi_, 0, :], in1=yy[:, pi_, 1, :], op=ALU.add),
                 reads=by, writes=Yr.b)
            p.op("dve", lambda e, pi_=pi_, fc=fc: e.tensor_tensor(out=Yr[:, 1, fc, :], in0=yy[:, pi_, 2, :], in1=yy[:, pi_, 3, :], op=ALU.subtract),
                 reads=by, writes=Yr.b)
    dump("Yr", Yr, [128, 2, NT, 512], BF16)
    yh = x0s
    bi_ = 0
    for tb in range(8):
        ti = tb % 2
        p.dma("sp", lambda e, tb=tb, ti=ti: e.dma_start(out=tabs[:, ti, 0, :, :], in_=dftC[tb]), writes=[tabs.b[ti]])
        p.dma("sp", lambda e, tb=tb, ti=ti: e.dma_start(out=tabs[:, ti, 1, :, :], in_=dftS[tb]), writes=[tabs.b[ti]])
        for cj in range(4):
            po = ps[bi_ % 4]
            bi_ += 1
            n_mm = 0
            NB_ = int(os.environ.get("DBG_B", "32"))
            for fc in range(NT):
                for cs_ in range(2):
                    if n_mm >= NB_:
                        continue
                    p.op("pe", lambda e, fc=fc, cs_=cs_, cj=cj, ti=ti, po=po, n_mm=n_mm: e.matmul(
                        po[:, 0:256], lhsT=Yr[:, cs_, fc, cj * 128:(cj + 1) * 128], rhs=tabs[:, ti, cs_, fc, :],
                        start=(n_mm == 0), stop=(n_mm == NB_ - 1)),
                         reads=Yr.b + [tabs.b[ti]], writes=po.b)
                    n_mm += 1
            if os.environ.get("DBG_Y"):
                p.op("dve", lambda e, po=po, cj=cj, tb=tb: e.tensor_copy(out=yh[:, cj, tb * 256:(tb + 1) * 256], in_=po[:, 0:256]),
                     reads=po.b + x0s.b, writes=yh.b)
                continue
            p.op("dve", lambda e, po=po, cj=cj, tb=tb: e.tensor_tensor(out=yh[:, cj, tb * 256:(tb + 1) * 256], in0=po[:, 0:256],
                                                                       in1=x0s[:, cj, tb * 256:(tb + 1) * 256], op=ALU.mult),
                 reads=po.b + x0s.b, writes=yh.b)
    dump("yh", yh, [128, 4, S], BF16)
    if stop_after <= 5:
        return finish(nc, p, dbg_out)

    p.barrier()
    lo = Arena(nc, base=m_persist, limit=m2)
    lo.n = 2000
    w_g = lo.alloc("w_g", [128, 8, 2048], BF16, nb=4)
    for c4 in range(4):
        p.dma("pool", lambda e, c4=c4: e.dma_start(out=w_g[:, :, c4 * 512:(c4 + 1) * 512],
                                                   in_=w_in[:, 1952 + c4 * 512:1952 + (c4 + 1) * 512].rearrange("(k p) n -> p k n", p=128)),
              writes=[w_g.b[c4]])
    w_o = lo.alloc("w_o", [128, 8, 1024], BF16, nb=2)
    for c2 in range(2):
        p.dma("pool", lambda e, c2=c2: e.dma_start(out=w_o[:, :, c2 * 512:(c2 + 1) * 512],
                                                   in_=w_out[:, c2 * 512:(c2 + 1) * 512].rearrange("(k p) n -> p k n", p=128)), writes=[w_o.b[c2]])
    w_oa = lo.alloc("w_oa", [128, 4, 1024], BF16)
    w_oh = lo.alloc("w_oh", [128, 4, 1024], BF16)
    p.dma("pool", lambda e: e.dma_start(out=w_oa[:], in_=w_o_attn.rearrange("(k p) n -> p k n", p=128)), writes=w_oa.b)
    p.dma("pool", lambda e: e.dma_start(out=w_oh[:], in_=w_o_hyena.rearrange("(k p) n -> p k n", p=128)), writes=w_oh.b)
    bg = lo.alloc("bg", [128, 2048], BF16)
    p.dma("pool", lambda e: e.dma_start(out=bg[0:1, :], in_=b_gate), writes=bg.b)
    ones = lo.alloc("ones", [128, 128], BF16)
    p.op("dve", lambda e: e.memset(ones[:], 1.0), writes=ones.b)
    g1b = lo.alloc("g1b", [128, 1024], F32)
    p.dma("sp", lambda e: e.dma_start(out=g1b[:], in_=attn_norm.broadcast_to([128, 1024])), writes=g1b.b)
    xt4 = lo.alloc("xt4", [128, 3, 1024], F32, nb=3)
    xn4 = lo.alloc("xn4", [128, 2, 1024], BF16, nb=2)
    xnT = lo.alloc("xnT", [128, 2, 8, 128], BF16, nb=2)
    aoT = lo.alloc("aoT", [128, 2, 4, 128], BF16, nb=2)
    sg = lo.alloc("sg", [128, 2, 512], F32, nb=2)
    mm_ = lo.alloc("mm", [128, 1024], F32)
    tmpm = lo.alloc("tmpm", [128, 2, 512], F32, nb=2)
    mb = lo.alloc("mb", [128, 2, 1024], BF16, nb=2)
    mT = lo.alloc("mT", [128, 8, 128], BF16)
    ho = lo.alloc("ho", [128, 1, 1024], F32, nb=1)

    def p4_front_a(t):
        i = t % 2
        i3 = t % 3
        src_ap = xt4[:, i3, :]
        p.dma("sp", lambda e: e.dma_start(out=xt4[:, i3, :], in_=x[t * 128:(t + 1) * 128, :]), writes=[xt4.b[i3]])
        p.op("act", lambda e: e.activation(out=junk[:], in_=src_ap, func=AF.Square, accum_out=st[:, t, 0:1]),
             reads=[xt4.b[i3]], writes=junk.b + st.b)
        p.op("act", lambda e: e.activation(out=st[:, t, 1:2], in_=st[:, t, 0:1], func=AF.Sqrt, scale=1.0 / 1024, bias=EPS),
             reads=st.b, writes=st.b)
        p.op("dve", lambda e: e.reciprocal(out=st[:, t, 2:3], in_=st[:, t, 1:2]), reads=st.b, writes=st.b)
        p.op("dve", lambda e: e.scalar_tensor_tensor(out=xn4[:, i, :], in0=src_ap, scalar=st[:, t, 2:3], in1=g1b[:],
                                                      op0=ALU.mult, op1=ALU.mult),
             reads=[xt4.b[i3]] + st.b + g1b.b, writes=[xn4.b[i]])

    def p4_front(t):
        i = t % 2
        pt0 = psb(0)
        for k in range(8):
            p.op("pe", lambda e, k=k: e.transpose(out=pt0[:, k * 128:(k + 1) * 128], in_=xn4[:, i, k * 128:(k + 1) * 128], identity=idb[:]),
                 reads=[xn4.b[i]] + idb.b, writes=ps[0].b)
        p.op("act", lambda e: e.copy(out=xnT[:, i, :, :], in_=pt0.rearrange("p (k n) -> p k n", k=8)), reads=ps[0].b, writes=[xnT.b[i]])
        pt1 = psb(1)
        for k4 in range(4):
            p.op("pe", lambda e, k4=k4: e.transpose(out=pt1[:, k4 * 128:(k4 + 1) * 128], in_=attn_o[:, t, k4 * 128:(k4 + 1) * 128], identity=idb[:]),
                 reads=attn_o.b + idb.b, writes=ps[1].b)
        p.op("act", lambda e: e.copy(out=aoT[:, i, :, :], in_=pt1[:, 0:512].rearrange("p (k n) -> p k n", k=4)), reads=ps[1].b, writes=[aoT.b[i]])

    def p4_mid(t):
        i = t % 2
        for br in range(2):
            for half in range(2):
                gcol = br * 1024 + half * 512
                pg = ps[2 + half]
                pv = ps[4 + half]
                for k in range(8):
                    p.op("pe", lambda e, k=k, gcol=gcol, pg=pg: e.matmul(pg[:, :], lhsT=xnT[:, i, k, :], rhs=w_g[:, k, gcol:gcol + 512], start=(k == 0), stop=False),
                         reads=[xnT.b[i], w_g.b[gcol // 512]], writes=pg.b)
                p.op("pe", lambda e, gcol=gcol, pg=pg: e.matmul(pg[:, :], lhsT=ones[0:1, :], rhs=bg[0:1, gcol:gcol + 512], start=False, stop=True),
                     reads=ones.b + bg.b, writes=pg.b)
                for k4 in range(4):
                    if br == 0:
                        p.op("pe", lambda e, k4=k4, half=half, pv=pv: e.matmul(pv[:, :], lhsT=aoT[:, i, k4, :], rhs=w_oa[:, k4, half * 512:(half + 1) * 512],
                                                                        start=(k4 == 0), stop=(k4 == 3)),
                             reads=[aoT.b[i]] + w_oa.b, writes=pv.b)
                    else:
                        p.op("pe", lambda e, k4=k4, half=half, pv=pv: e.matmul(pv[:, :], lhsT=yh[:, k4, t * 128:(t + 1) * 128],
                                                                        rhs=w_oh[:, k4, half * 512:(half + 1) * 512], start=(k4 == 0), stop=(k4 == 3)),
                             reads=yh.b + w_oh.b, writes=pv.b)
                p.op("act", lambda e, half=half, pg=pg: e.activation(out=sg[:, half, :], in_=pg[:, :], func=AF.Sigmoid), reads=pg.b, writes=[sg.b[half]])
                if br == 0:
                    p.op("dve", lambda e, half=half, pv=pv: e.tensor_tensor(out=mm_[:, half * 512:(half + 1) * 512], in0=pv[:, :], in1=sg[:, half, :], op=ALU.mult),
                         reads=pv.b + [sg.b[half]], writes=mm_.b)
                else:
                    p.op("dve", lambda e, half=half, pv=pv: e.tensor_tensor(out=tmpm[:, half, :], in0=pv[:, :], in1=sg[:, half, :], op=ALU.mult),
                         reads=pv.b + [sg.b[half]], writes=[tmpm.b[half]])
                    p.op("dve", lambda e, half=half: e.tensor_tensor(out=mb[:, i, half * 512:(half + 1) * 512], in0=mm_[:, half * 512:(half + 1) * 512],
                                                                     in1=tmpm[:, half, :], op=ALU.add),
                         reads=mm_.b + [tmpm.b[half]], writes=[mb.b[i]])

    def p4_back(t):
        i = t % 2
        pt6 = psb(6)
        for k in range(8):
            p.op("pe", lambda e, k=k: e.transpose(out=pt6[:, k * 128:(k + 1) * 128], in_=mb[:, i, k * 128:(k + 1) * 128], identity=idb[:]),
                 reads=[mb.b[i]] + idb.b, writes=ps[6].b)
        p.op("act", lambda e: e.copy(out=mT[:], in_=pt6.rearrange("p (k n) -> p k n", k=8)), reads=ps[6].b, writes=mT.b)
        for half in range(2):
            pw = ps[7]
            for k in range(8):
                p.op("pe", lambda e, k=k, half=half, pw=pw: e.matmul(pw[:, :], lhsT=mT[:, k, :], rhs=w_o[:, k, half * 512:(half + 1) * 512],
                                                              start=(k == 0), stop=(k == 7)),
                     reads=mT.b + [w_o.b[half]], writes=pw.b)
            p.op("dve", lambda e, half=half, pw=pw: e.tensor_tensor(out=ho[:, 0, half * 512:(half + 1) * 512], in0=pw[:, :],
                                                               in1=xt4[:, t % 3, half * 512:(half + 1) * 512], op=ALU.add),
                 reads=pw.b + [xt4.b[t % 3]], writes=ho.b)
        p.dma("sp", lambda e: e.dma_start(out=out[t * 128:(t + 1) * 128, :], in_=ho[:, 0, :]), reads=ho.b, writes=[b_out[t]])

    def p4_fm(t):
        p4_front(t)
        p4_mid(t)

    p4_front_a(0)
    p4_front_a(1)
    p4_fm(0)
    for t in range(NT):
        recs = [record(p4_back, t)]
        if t + 1 < NT:
            recs.append(record(p4_fm, t + 1))
        if t + 2 < NT:
            recs.append(record(p4_front_a, t + 2))
        interleave(*recs)
    if stop_after <= 6:
        return finish(nc, p, dbg_out)

    p.barrier()
    ar = Arena(nc, base=m_persist)
    ar.n = 3000
    w_q = ar.alloc("w_q", [128, 8, 2048], BF16, nb=4)
    for c4 in range(4):
        p.dma("pool", lambda e, c4=c4: e.dma_start(out=w_q[:, :, c4 * 512:(c4 + 1) * 512],
                                                   in_=peer_w_q[:, c4 * 512:(c4 + 1) * 512].rearrange("(k p) n -> p k n", p=128)), writes=[w_q.b[c4]])
    kTs = ar.alloc("kTs", [128, 16, 128], F32)
    p.dma("sp", lambda e: e.dma_start(out=kTs[:], in_=keysT), writes=kTs.b)
    g2 = ar.alloc("g2", [128, 1024], F32)
    p.dma("sp", lambda e: e.dma_start(out=g2[:], in_=ffn_norm.broadcast_to([128, 1024])), writes=g2.b)
    io16 = ar.alloc("io16", [128, 256], F32)
    p.dma("sp", lambda e: e.dma_start(out=io16[:], in_=iota16), writes=io16.b)
    ht = ar.alloc("ht", [128, 2, 1024], F32, nb=2)
    ei = ar.alloc("ei", [128, 2, 128], I32, nb=2)
    gw = ar.alloc("gw", [128, 2, 128], F32, nb=2)
    hnb2 = ar.alloc("hnb2", [128, 2, 1024], BF16, nb=2)
    prod = None
    junka = None
    hnT = ar.alloc("hnT", [128, 8, 128], BF16)
    qTs = ar.alloc("qTs", [128, 16, 128], F32)
    sc = ar.alloc("sc", [128, 16, 128], F32)
    scw = ar.alloc("scw", [128, 16, 128], F32, nb=16)
    v16 = ar.alloc("v16", [128, 16, 16], F32, nb=16)
    i16u = ar.alloc("i16u", [128, 16, 16], U32, nb=16)
    i16f = ar.alloc("i16f", [128, 16, 16], F32)
    cand = ar.alloc("cand", [128, 8, 256], F32)
    candw = ar.alloc("candw", [128, 8, 256], F32, nb=8)
    ts_ = ar.alloc("ts", [128, 8, 16], F32, nb=8)
    posu = ar.alloc("posu", [128, 8, 16], U32, nb=8)
    phu = ar.alloc("phu", [128, 2, 8, 16], U32)
    phf = ar.alloc("phf", [128, 2, 8, 16], F32)
    oh = ar.alloc("oh", [128, 8, 16, 16], F32)
    sel = ar.alloc("sel", [128, 2, 8, 16], F32)
    ef = ar.alloc("ef", [128, 128], F32)
    sm = ar.alloc("sm", [128, 4, 8], F32)
    ex = ar.alloc("ex", [128, 8, 16], F32)
    da = ar.alloc("da", [128, 2, 128], F32, nb=4)
    dsum = ar.alloc("dsum", [128, 128], F32, nb=16)
    actv = ar.alloc("actv", [128, 128], F32, nb=16)
    wgt = ar.alloc("wgt", [128, 128], F32, nb=16)
    junkb = ar.alloc("junkb", [128, 4, 512], BF16, nb=4)
    NB = 19
    ub = ar.alloc("ub", [128, NB, 2048], BF16, nb=NB)
    ND = 8
    dg = ar.alloc("dg", [128, ND, 128], BF16, nb=ND)
    fo = junk

    def make_steps(t):
        i = t % 2
        hp = [ps[2 * i], ps[2 * i + 1]]
        src = ht[:, i, :]
        pt = psb(6)
        v4 = v16[:].rearrange("p (h s) k -> p h s k", s=2)
        i4 = i16f[:].rearrange("p (h s) k -> p h s k", s=2)
        cand4 = cand[:].rearrange("p h (a b) -> p h a b", a=16)

        def s0():
            p.dma("sp", lambda e: e.dma_start(out=ht[:, i, :], in_=out[t * 128:(t + 1) * 128, :]), reads=[b_out[t]], writes=[ht.b[i]])
            p.op("act", lambda e: e.activation(out=junk[:], in_=src, func=AF.Square, accum_out=st[:, t, 0:1]), reads=[ht.b[i]], writes=junk.b + st.b)
            p.op("act", lambda e: e.activation(out=st[:, t, 1:2], in_=st[:, t, 0:1], func=AF.Sqrt, scale=1.0 / 1024, bias=EPS), reads=st.b, writes=st.b)

        def s1():
            p.op("dve", lambda e: e.reciprocal(out=st[:, t, 2:3], in_=st[:, t, 1:2]), reads=st.b, writes=st.b)
            for half in range(2):
                p.op("dve", lambda e, half=half: e.scalar_tensor_tensor(out=hp[half][:, :], in0=ht[:, i, half * 512:(half + 1) * 512], scalar=st[:, t, 2:3],
                                                                       in1=g2[:, half * 512:(half + 1) * 512], op0=ALU.mult, op1=ALU.mult),
                     reads=[ht.b[i]] + st.b + g2.b, writes=hp[half].b)
                p.op("dve", lambda e, half=half: e.tensor_copy(out=hnb2[:, i, half * 512:(half + 1) * 512], in_=hp[half][:, :]), reads=hp[half].b, writes=[hnb2.b[i]])

        def s2():
            for k in range(8):
                p.op("pe", lambda e, k=k: e.transpose(out=pt[:, k * 128:(k + 1) * 128], in_=hnb2[:, i, k * 128:(k + 1) * 128], identity=idb[:]),
                     reads=[hnb2.b[i]] + idb.b, writes=ps[6].b)

        def s3():
            p.op("act", lambda e: e.copy(out=hnT[:], in_=pt.rearrange("p (k n) -> p k n", k=8)), reads=ps[6].b, writes=hnT.b)

        def qmm(rnd):
            def f():
                for c8 in range(8):
                    c = rnd * 8 + c8
                    pq = ps[6 + c8 // 4]
                    for k in range(8):
                        p.op("pe", lambda e, c=c, c8=c8, k=k, pq=pq: e.matmul(pq[:, (c8 % 4) * 128:(c8 % 4 + 1) * 128], lhsT=w_q[:, k, c * 128:(c + 1) * 128],
                                                                      rhs=hnT[:, k, :], start=(k == 0), stop=(k == 7)),
                             reads=w_q.b + hnT.b, writes=pq.b)
            return f

        def qcp(rnd, dst):
            def f():
                for b2 in range(2):
                    p.op("act", lambda e, b2=b2: e.copy(out=dst[:, rnd * 8 + b2 * 4:rnd * 8 + (b2 + 1) * 4, :],
                                                        in_=ps[6 + b2][:, :].rearrange("p (c n) -> p c n", c=4)),
                         reads=ps[6 + b2].b, writes=dst.b)
            return f

        def smm(rnd):
            def f():
                for c8 in range(8):
                    c = rnd * 8 + c8
                    pq = ps[6 + c8 // 4]
                    p.op("pe", lambda e, c=c, c8=c8, pq=pq: e.matmul(pq[:, (c8 % 4) * 128:(c8 % 4 + 1) * 128], lhsT=qTs[:, c, :], rhs=kTs[:, c, :], start=True, stop=True),
                         reads=qTs.b + kTs.b, writes=pq.b)
            return f

        def k1():
            for c in range(16):
                p.op("dve", lambda e, c=c: e.max(out=v16[:, c, 0:8], in_=sc[:, c, :]), reads=sc.b, writes=[v16.b[c]])
            for c in range(16):
                p.op("dve", lambda e, c=c: e.max_index(out=i16u[:, c, 0:8], in_max=v16[:, c, 0:8], in_values=sc[:, c, :]), reads=sc.b + [v16.b[c]], writes=[i16u.b[c]])

        def k2():
            for c in range(16):
                p.op("dve", lambda e, c=c: e.match_replace(out=scw[:, c, :], in_to_replace=v16[:, c, 0:8], in_values=sc[:, c, :], imm_value=-1e30),
                     reads=sc.b + [v16.b[c]], writes=[scw.b[c]])
            for c in range(16):
                p.op("dve", lambda e, c=c: e.max(out=v16[:, c, 8:16], in_=scw[:, c, :]), reads=[scw.b[c]], writes=[v16.b[c]])

        def k3():
            for c in range(16):
                p.op("dve", lambda e, c=c: e.max_index(out=i16u[:, c, 8:16], in_max=v16[:, c, 8:16], in_values=scw[:, c, :]), reads=[scw.b[c], v16.b[c]], writes=[i16u.b[c]])
            p.op("dve", lambda e: e.tensor_copy(out=i16f[:], in_=i16u[:]), reads=i16u.b, writes=i16f.b)
            p.op("dve", lambda e: e.tensor_tensor(out=cand4, in0=bc(v4[:, :, 0, :], 3, [128, 8, 16, 16]), in1=bc(v4[:, :, 1, :], 2, [128, 8, 16, 16]), op=ALU.add),
                 reads=v16.b, writes=cand.b)

        def k4():
            for h in range(8):
                p.op("dve", lambda e, h=h: e.max(out=ts_[:, h, 0:8], in_=cand[:, h, :]), reads=cand.b, writes=[ts_.b[h]])
            for h in range(8):
                p.op("dve", lambda e, h=h: e.max_index(out=posu[:, h, 0:8], in_max=ts_[:, h, 0:8], in_values=cand[:, h, :]), reads=cand.b + [ts_.b[h]], writes=[posu.b[h]])
            for h in range(8):
                p.op("dve", lambda e, h=h: e.match_replace(out=candw[:, h, :], in_to_replace=ts_[:, h, 0:8], in_values=cand[:, h, :], imm_value=-1e30),
                     reads=cand.b + [ts_.b[h]], writes=[candw.b[h]])

        def k5():
            for h in range(8):
                p.op("dve", lambda e, h=h: e.max(out=ts_[:, h, 8:16], in_=candw[:, h, :]), reads=[candw.b[h]], writes=[ts_.b[h]])
            for h in range(8):
                p.op("dve", lambda e, h=h: e.max_index(out=posu[:, h, 8:16], in_max=ts_[:, h, 8:16], in_values=candw[:, h, :]), reads=[candw.b[h], ts_.b[h]], writes=[posu.b[h]])
            p.op("dve", lambda e: e.tensor_scalar(out=phu[:, 0, :, :], in0=posu[:], scalar1=4, scalar2=None, op0=ALU.logical_shift_right), reads=posu.b, writes=phu.b)
            p.op("dve", lambda e: e.tensor_scalar(out=phu[:, 1, :, :], in0=posu[:], scalar1=15, scalar2=None, op0=ALU.bitwise_and), reads=posu.b, writes=phu.b)
            p.op("dve", lambda e: e.tensor_copy(out=phf[:], in_=phu[:]), reads=phu.b, writes=phf.b)
            p.op("dve", lambda e: e.tensor_tensor(out=ex[:], in0=ts_[:], in1=bc(ts_[:, :, 0], 2, [128, 8, 16]), op=ALU.subtract), reads=ts_.b, writes=ex.b)
            p.op("act", lambda e: e.activation(out=ex[:], in_=ex[:], func=AF.Exp), reads=ex.b, writes=ex.b)

        def k6():
            for s_ in range(2):
                p.op("dve", lambda e, s_=s_: e.tensor_tensor(out=oh[:], in0=bc(phf[:, s_, :, :], 3, [128, 8, 16, 16]),
                                                             in1=bc(io16[:].rearrange("p (a b) -> p a b", a=16), 1, [128, 8, 16, 16]), op=ALU.is_equal),
                     reads=phf.b + io16.b, writes=oh.b)
                p.op("dve", lambda e, s_=s_: e.tensor_tensor(out=oh[:], in0=oh[:], in1=bc(i4[:, :, s_, :], 2, [128, 8, 16, 16]), op=ALU.mult),
                     reads=oh.b + i16f.b, writes=oh.b)
                p.op("dve", lambda e, s_=s_: e.tensor_reduce(out=sel[:, s_, :, :], in_=oh[:], axis=AXX, op=ALU.add), reads=oh.b, writes=sel.b)
            p.op("dve", lambda e: e.scalar_tensor_tensor(out=ef[:], in0=sel[:, 0, :, :].rearrange("p h k -> p (h k)"), scalar=128.0,
                                                          in1=sel[:, 1, :, :].rearrange("p h k -> p (h k)"), op0=ALU.mult, op1=ALU.add),
                 reads=sel.b, writes=ef.b)
            p.op("dve", lambda e: e.tensor_copy(out=ei[:, i, :], in_=ef[:]), reads=ef.b, writes=[ei.b[i]])
            p.op("dve", lambda e: e.tensor_reduce(out=sm[:, 0, :], in_=ex[:], axis=AXX, op=ALU.add), reads=ex.b, writes=sm.b)
            p.op("dve", lambda e: e.reciprocal(out=sm[:, 1, :], in_=sm[:, 0, :]), reads=sm.b, writes=sm.b)
            p.op("dve", lambda e: e.tensor_tensor(out=gw[:, i, :].rearrange("p (h k) -> p h k", h=8), in0=ex[:], in1=bc(sm[:, 1, :], 2, [128, 8, 16]), op=ALU.mult),
                 reads=ex.b + sm.b, writes=[gw.b[i]])

        return {0: [s0], 1: [s1], 2: [s2], 3: [s3], 4: [qmm(0)], 5: [qcp(0, qTs), qmm(1)], 6: [qcp(1, qTs)], 7: [smm(0)],
                8: [qcp(0, sc), smm(1)], 9: [qcp(1, sc)], 10: [k1], 11: [k2], 12: [k3], 13: [k4], 14: [k5], 15: [k6]}

    gctr = [0, 0, 0, 0]
    NPO = int(os.environ.get("NPO", "0"))
    first_gather = [True]
    n_peer = NT if stop_after > 7 else 1

    GS = 4
    NG = 128 // GS

    def stageBC(t):
        i = t % 2
        nxt = make_steps(t + 1) if t + 1 < n_peer else None
        hp = [ps[2 * i], ps[2 * i + 1]]
        jmap = {}

        def tail(g):
            gs = slice(g * GS, (g + 1) * GS)
            gb = g % 16
            p.op("dve", lambda e: e.tensor_tensor(out=wgt[:, gs], in0=actv[:, gs], in1=gw[:, i, gs], op=ALU.mult),
                 reads=[actv.b[gb], gw.b[i]], writes=[wgt.b[gb]])
            for s4 in range(GS):
                s_ = g * GS + s4
                j = jmap[s_]
                d = gctr[2] % ND
                gctr[2] += 1
                p.op("act", lambda e, d=d, s_=s_: e.activation(out=dg[:, d, :], in_=idb[:], func=AF.Copy, scale=wgt[:, s_:s_ + 1]),
                     reads=idb.b + [wgt.b[gb]], writes=[dg.b[d]])
                for half in range(2):
                    p.op("pe", lambda e, d=d, j=j, half=half, s_=s_: e.matmul(ps[4 + half][:, :], lhsT=dg[:, d, :],
                                                                          rhs=ub[:, j, 1024 + half * 512:1024 + (half + 1) * 512],
                                                                          start=(s_ == 0), stop=(s_ == 127)),
                         reads=[dg.b[d], ub.b[j]], writes=ps[4 + half].b)

        for g in range(NG):
            gs = slice(g * GS, (g + 1) * GS)
            gb = g % 16
            for s4 in range(GS):
                s_ = g * GS + s4
                j = gctr[0] % NB
                gctr[0] += 1
                jmap[s_] = j
                rd = [ei.b[i]]
                if first_gather[0]:
                    rd = rd + b_uvb
                    first_gather[0] = False
                p.dma("pool", lambda e, j=j, s_=s_: e.indirect_dma_start(out=ub[:, j, :], out_offset=None, in_=uvb,
                                                                         in_offset=bass.IndirectOffsetOnAxis(ap=ei[:, i, s_:s_ + 1], axis=0)),
                      reads=rd, writes=[ub.b[j]])
                r4 = gctr[1] % 2
                gctr[1] += 1
                p.op("dve", lambda e, j=j, s_=s_, r4=r4: e.scalar_tensor_tensor(
                    out=junkb[:, 2 * r4:2 * r4 + 2, :].rearrange("p a b -> p (a b)"), in0=ub[:, j, 0:1024], scalar=1.0, in1=hnp[i][:, :],
                    op0=ALU.mult, op1=ALU.mult, accum_out=da[:, 0, s_:s_ + 1]),
                     reads=[ub.b[j]] + hp[0].b + hp[1].b, writes=[junkb.b[r4], da.b[r4]])
            p.op("act", lambda e, gs=gs: e.activation(out=actv[:, gs], in_=da[:, 0, gs], func=AF.Gelu), reads=da.b, writes=[actv.b[gb]])
            if g >= 1:
                tail(g - 1)
            if nxt is not None and g % 2 == 1:
                for f_ in nxt[g // 2]:
                    f_()
        tail(NG - 1)
        for half in range(2):
            p.op("dve", lambda e, half=half: e.tensor_tensor(out=fo[:, half * 512:(half + 1) * 512], in0=ps[4 + half][:, :],
                                                             in1=ht[:, i, half * 512:(half + 1) * 512], op=ALU.add),
                 reads=ps[4 + half].b + [ht.b[i]], writes=fo.b)
        p.dma("sp", lambda e: e.dma_start(out=out[t * 128:(t + 1) * 128, :], in_=fo[:]), reads=fo.b, writes=[b_out[t]])

    st0 = make_steps(0)
    for gi_ in range(16):
        for f_ in st0[gi_]:
            f_()
    for t in range(n_peer):
        stageBC(t)
    return finish(nc, p, dbg_out)


def finish(nc, p, dbg_out):
    p.emit()
    return nc, dbg_out


_CONST = {}


def host_consts():
    if _CONST:
        return _CONST
    c = _CONST
    c["identb"] = np.eye(128, dtype=np.float32).astype(ml_dtypes.bfloat16)
    c["identf"] = np.eye(128, dtype=np.float32)
    half = 16
    inv = 10000.0 ** (-np.arange(half, dtype=np.float64) / half)
    ang = np.arange(S, dtype=np.float64)[:, None] * inv[None, :]
    c["rot_cos"] = np.cos(ang).astype(np.float32)
    c["rot_sin"] = np.sin(ang).astype(np.float32)
    L = S
    t = np.linspace(0.0, 1.0, L, dtype=np.float64)[:, None]
    w = 2.0 * math.pi * np.arange(L, dtype=np.float64)[:, None] / L
    f = np.linspace(1e-4, 15, 16, dtype=np.float64)[None, :]
    z = np.concatenate([t, np.cos(f * w), -np.sin(f * w)], axis=-1)
    c["zfeat"] = np.ascontiguousarray(z.T).astype(np.float32)
    min_decay = math.log(1e-2) / 1.5
    max_decay = math.log(1e-2) / 0.3
    deltas = np.abs(np.linspace(min_decay, max_decay, 512, dtype=np.float64))
    decay = np.exp(-t * deltas[None, :])
    c["decay_f"] = decay.astype(np.float32)
    db = np.zeros_like(decay)
    db[:-1] = decay[1:]
    c["decay_b"] = db.astype(np.float32)
    n = np.arange(S, dtype=np.float64)
    th = 2 * np.pi * np.outer(n + 0.5, n + 0.5) / (2 * S)
    for nm, M in (("dftC", np.cos(th)), ("dftS", np.sin(th))):
        T = M.reshape(16, 128, 8, 256).transpose(2, 1, 0, 3)
        c[nm] = np.ascontiguousarray(T).astype(ml_dtypes.bfloat16)
    pc = np.cos(np.pi * (n + 0.5) / (2 * S)) / S
    ps_ = np.sin(np.pi * (n + 0.5) / (2 * S)) / S
    ph = np.stack([pc, ps_], -1).reshape(16, 128, 2).transpose(1, 0, 2)
    c["phase"] = np.ascontiguousarray(ph).astype(np.float32)
    c["iota16"] = np.broadcast_to(np.tile(np.arange(16, dtype=np.float32), 16)[None, :], (128, 256)).copy()
    return c


def make_in_maps(I):
    f = lambda a: np.ascontiguousarray(np.asarray(a, dtype=np.float32))
    c = host_consts()
    shared = dict(c)
    shared["attn_norm"] = f(I["attn_norm"][0][None, :])
    shared["w_in"] = f(I["w_in"][0])
    shared["b_gate"] = f(I["b_gate"][0][None, :])
    shared["q_a_norm"] = f(I["q_a_norm"][0][None, :])
    shared["w_uq"] = f(I["w_uq"][0])
    shared["kv_a_norm"] = f(I["kv_a_norm"][0][None, :])
    shared["w_ukv"] = f(I["w_ukv"][0])
    shared["q_norm"] = f(I["q_norm"][0][None, :])
    shared["k_norm"] = f(I["k_norm"][0][None, :])
    shared["w_o_attn"] = f(I["w_o_attn"][0])
    cw = np.asarray(I["hyena_conv_w"][0])
    shared["conv_w"] = f(cw.T.reshape(12, 128, 3).transpose(1, 0, 2))
    shared["conv_b"] = f(np.asarray(I["hyena_conv_b"][0]).reshape(12, 128).T)
    shared["filt_w1"] = f(I["filt_w1"][0])
    shared["filt_w2"] = f(I["filt_w2"][0])
    shared["filt_w3"] = f(I["filt_w3"][0])
    shared["filt_b"] = f(np.stack([np.asarray(I["filt_b1"][0]), np.asarray(I["filt_b2"][0]), np.asarray(I["filt_b3"][0]),
                                   np.asarray(I["filt_freq"][0])], -1))
    shared["filt_w4a"] = f(np.concatenate([np.asarray(I["filt_w4"][0]), np.asarray(I["filt_b4"][0])[None, :]], 0))
    shared["hy_bias"] = f(I["hyena_bias"][0][None, :])
    shared["w_o_hyena"] = f(I["w_o_hyena"][0])
    shared["w_out"] = f(I["w_out"][0])
    shared["ffn_norm"] = f(I["ffn_norm"][0][None, :])
    shared["peer_w_q"] = f(I["peer_w_q"][0])
    k1 = np.asarray(I["peer_keys1"][0])
    k2 = np.asarray(I["peer_keys2"][0])
    kk = np.stack([k1, k2], 1).reshape(16, 128, 128)
    shared["keysT"] = f(kk.transpose(2, 0, 1))
    shared["expert_u"] = f(I["expert_u"][0])
    shared["expert_v"] = f(I["expert_v"][0])
    xs = np.asarray(I["x"], dtype=np.float32)
    maps = []
    for b in range(8):
        m = dict(shared)
        m["x"] = np.ascontiguousarray(xs[b])
        maps.append(m)
    return maps


_NC_CACHE = {}


def kernel(**inputs):
    if "nc" not in _NC_CACHE:
        _NC_CACHE["nc"] = build_nc()[0]
    nc = _NC_CACHE["nc"]
    in_maps = make_in_maps(inputs)
    res = run_bass_kernel_spmd(nc, in_maps, core_ids=list(range(8)))
    return np.stack([np.asarray(r["out"], dtype=np.float32) for r in res.results], 0)
```
